# Optimizing a Trainium2 kernel written in Bass

```python
import jax, jax.numpy as jnp
from jax import lax
import numpy as np

D_MODEL = 1024
BATCH = 8
SEQ = 2048
DEPTH = 1

LN_EPS = 1e-5
RMS_EPS = 1e-6

GLA_HEADS = 4
GLA_DK = D_MODEL // 2
GLA_DV = D_MODEL
GLA_HK = GLA_DK // GLA_HEADS
GLA_HV = GLA_DV // GLA_HEADS
GLA_GATE_RANK = 16
GLA_TAU = 16.0
GLA_CHUNK = 64

MLA_HEADS = 8
MLA_Q_RANK = 384
MLA_KV_RANK = 256
MLA_NOPE = 128
MLA_ROPE = 64
MLA_V = 128
MLA_QK = MLA_NOPE + MLA_ROPE
ROPE_THETA = 10000.0
Q_BLOCK = 128

D_FF = 4 * D_MODEL

N_BRANCH = 2
DEEPNORM_ALPHA = (2.0 * DEPTH) ** 0.25
DEEPNORM_BETA = (8.0 * DEPTH) ** -0.25

IN_SPLITS = (GLA_DK, GLA_DK, GLA_DV, GLA_DV, GLA_GATE_RANK, MLA_Q_RANK, MLA_KV_RANK + MLA_ROPE, N_BRANCH * D_MODEL)
D_IN = sum(IN_SPLITS)

kernel_name = 'hybrid_gla_mla_deepnorm_block'


def layer_norm(x, g, b):
    xf = x.astype(jnp.float32)
    mu = jnp.mean(xf, axis=-1, keepdims=True)
    var = jnp.mean(jnp.square(xf - mu), axis=-1, keepdims=True)
    return ((xf - mu) * lax.rsqrt(var + LN_EPS) * g + b).astype(x.dtype)


def rms_norm(x, g):
    xf = x.astype(jnp.float32)
    return (xf * lax.rsqrt(jnp.mean(jnp.square(xf), axis=-1, keepdims=True) + RMS_EPS) * g).astype(x.dtype)


def rope_tables(positions):
    inv_freq = 1.0 / (ROPE_THETA ** (jnp.arange(0, MLA_ROPE, 2, dtype=jnp.float32) / MLA_ROPE))
    ang = positions.astype(jnp.float32)[..., None] * inv_freq
    return jnp.cos(ang), jnp.sin(ang)


def apply_rope(x, cos, sin):
    half = x.shape[-1] // 2
    xf = x.astype(jnp.float32)
    x1, x2 = xf[..., :half], xf[..., half:]
    return jnp.concatenate([x1 * cos - x2 * sin, x2 * cos + x1 * sin], axis=-1).astype(x.dtype)


def split_heads(t, n_heads):
    b, s, _ = t.shape
    return t.reshape(b, s, n_heads, -1).transpose(0, 2, 1, 3)


def merge_heads(t):
    b, h, s, d = t.shape
    return t.transpose(0, 2, 1, 3).reshape(b, s, h * d)


def gla_chunked(q, k, v, log_a):
    bsz, nh, seq, dk = q.shape
    dv = v.shape[-1]
    n_chunks = seq // GLA_CHUNK

    def to_chunks(t):
        return t.reshape(bsz, nh, n_chunks, GLA_CHUNK, t.shape[-1]).transpose(2, 0, 1, 3, 4)

    causal = jnp.tril(jnp.ones((GLA_CHUNK, GLA_CHUNK), dtype=bool))

    def step(state, inp):
        qi, ki, vi, gi = inp
        b = jnp.cumsum(gi, axis=2)
        b_last = b[:, :, -1:, :]
        diff = b[:, :, :, None, :] - b[:, :, None, :, :]
        decay = jnp.exp(jnp.where(causal[:, :, None], diff, -jnp.inf))
        scores = jnp.einsum('bhid,bhjd,bhijd->bhij', qi, ki, decay)
        o = jnp.einsum('bhij,bhjv->bhiv', scores, vi) + jnp.einsum('bhid,bhdv->bhiv', qi * jnp.exp(b), state)
        state = jnp.exp(b_last)[:, :, 0, :, None] * state + jnp.einsum('bhjd,bhjv->bhdv', ki * jnp.exp(b_last - b), vi)
        return state, o

    state0 = jnp.zeros((bsz, nh, dk, dv), jnp.float32)
    _, o = lax.scan(step, state0, (to_chunks(q), to_chunks(k), to_chunks(v), to_chunks(log_a)))
    return o.transpose(1, 2, 0, 3, 4).reshape(bsz, nh, seq, dv)


def mla_causal_attention(q_nope, q_rope, k_nope, k_rope, v):
    seq = q_nope.shape[2]
    scale = MLA_QK ** -0.5
    outs = []
    for blk in range(seq // Q_BLOCK):
        q0, q1 = blk * Q_BLOCK, (blk + 1) * Q_BLOCK
        s = (jnp.einsum('bhqd,bhkd->bhqk', q_nope[:, :, q0:q1], k_nope[:, :, :q1])
             + jnp.einsum('bhqr,bkr->bhqk', q_rope[:, :, q0:q1], k_rope[:, :q1]))
        s = s.astype(jnp.float32) * scale
        mask = (q0 + jnp.arange(Q_BLOCK))[:, None] >= jnp.arange(q1)[None, :]
        p = jax.nn.softmax(jnp.where(mask, s, -jnp.inf), axis=-1).astype(v.dtype)
        outs.append(jnp.einsum('bhqk,bhkv->bhqv', p, v[:, :, :q1]))
    return jnp.concatenate(outs, axis=2)


def token_mixer(h, cos, sin, w_in, w_gla_a2, b_gla_a2, gla_norm_g, w_o_gla,
                q_a_norm_g, w_q_b, kv_a_norm_g, w_kv_b, w_o_mla, b_gate, w_out):
    bsz, seq, _ = h.shape
    offsets = np.cumsum(IN_SPLITS)[:-1].tolist()
    q_g, k_g, v_g, r_g, a_lr, q_lat, kv_lat, gate_logits = jnp.split(h @ w_in, offsets, axis=-1)

    log_a = jax.nn.log_sigmoid((a_lr @ w_gla_a2 + b_gla_a2).astype(jnp.float32)) / GLA_TAU
    o_gla = gla_chunked(split_heads(q_g.astype(jnp.float32), GLA_HEADS) * GLA_HK ** -0.5,
                        split_heads(k_g.astype(jnp.float32), GLA_HEADS),
                        split_heads(v_g.astype(jnp.float32), GLA_HEADS),
                        split_heads(log_a, GLA_HEADS))
    o_gla = rms_norm(o_gla.transpose(0, 2, 1, 3), gla_norm_g.reshape(GLA_HEADS, GLA_HV))
    o_gla = o_gla.reshape(bsz, seq, GLA_DV).astype(h.dtype)
    y_gla = (o_gla * jax.nn.silu(r_g)) @ w_o_gla

    q = (rms_norm(q_lat, q_a_norm_g) @ w_q_b).reshape(bsz, seq, MLA_HEADS, MLA_QK)
    q_nope = q[..., :MLA_NOPE]
    q_rope = apply_rope(q[..., MLA_NOPE:], cos[:, :, None], sin[:, :, None])
    c_kv = rms_norm(kv_lat[..., :MLA_KV_RANK], kv_a_norm_g)
    k_rope = apply_rope(kv_lat[..., MLA_KV_RANK:], cos, sin)
    kv = (c_kv @ w_kv_b).reshape(bsz, seq, MLA_HEADS, MLA_NOPE + MLA_V)
    k_nope, v = kv[..., :MLA_NOPE], kv[..., MLA_NOPE:]
    o_mla = mla_causal_attention(q_nope.transpose(0, 2, 1, 3), q_rope.transpose(0, 2, 1, 3),
                                 k_nope.transpose(0, 2, 1, 3), k_rope, v.transpose(0, 2, 1, 3))
    y_mla = merge_heads(o_mla) @ w_o_mla

    g_gla, g_mla = jnp.split(jax.nn.sigmoid(gate_logits + b_gate), N_BRANCH, axis=-1)
    return (g_gla * y_gla + g_mla * y_mla) @ w_out


def setup_inputs(seed: int = 0) -> dict:
    key = jax.random.key(seed)
    ks = jax.random.split(key, 24)
    f32 = jnp.float32

    def normal(k, shape, scale):
        return jax.random.normal(k, shape, f32) * scale

    def gain(k, shape):
        return 1.0 + 0.02 * jax.random.normal(k, shape, f32)

    beta = DEEPNORM_BETA
    x = jax.random.normal(ks[0], (BATCH, SEQ, D_MODEL), f32)
    offset = jax.random.randint(ks[1], (BATCH, 1), 0, 4096, dtype=jnp.int32)
    positions = offset + jnp.arange(SEQ, dtype=jnp.int32)[None, :]
    v_lo = 2 * GLA_DK
    in_col_scale = jnp.ones((D_IN,), f32).at[v_lo:v_lo + GLA_DV].set(beta)
    kv_col_scale = jnp.tile(jnp.concatenate([jnp.ones((MLA_NOPE,), f32), jnp.full((MLA_V,), beta, f32)]), MLA_HEADS)
    return {
        'x': x,
        'positions': positions,
        'ln_in_g': gain(ks[2], (D_MODEL,)),
        'ln_in_b': normal(ks[3], (D_MODEL,), 0.02),
        'w_in': normal(ks[4], (DEPTH, D_MODEL, D_IN), D_MODEL ** -0.5) * in_col_scale,
        'w_gla_a2': normal(ks[5], (DEPTH, GLA_GATE_RANK, GLA_DK), GLA_GATE_RANK ** -0.5),
        'b_gla_a2': normal(ks[6], (DEPTH, GLA_DK), 0.1),
        'gla_norm_g': gain(ks[7], (DEPTH, GLA_DV)),
        'w_o_gla': normal(ks[8], (DEPTH, GLA_DV, D_MODEL), beta * GLA_DV ** -0.5),
        'q_a_norm_g': gain(ks[9], (DEPTH, MLA_Q_RANK)),
        'w_q_b': normal(ks[10], (DEPTH, MLA_Q_RANK, MLA_HEADS * MLA_QK), MLA_Q_RANK ** -0.5),
        'kv_a_norm_g': gain(ks[11], (DEPTH, MLA_KV_RANK)),
        'w_kv_b': normal(ks[12], (DEPTH, MLA_KV_RANK, MLA_HEADS * (MLA_NOPE + MLA_V)), MLA_KV_RANK ** -0.5) * kv_col_scale,
        'w_o_mla': normal(ks[13], (DEPTH, MLA_HEADS * MLA_V, D_MODEL), beta * (MLA_HEADS * MLA_V) ** -0.5),
        'b_gate': normal(ks[14], (DEPTH, N_BRANCH * D_MODEL), 0.1),
        'w_out': normal(ks[15], (DEPTH, D_MODEL, D_MODEL), beta * D_MODEL ** -0.5),
        'ln1_g': gain(ks[16], (DEPTH, D_MODEL)),
        'ln1_b': normal(ks[17], (DEPTH, D_MODEL), 0.02),
        'w_ff1': normal(ks[18], (DEPTH, D_MODEL, D_FF), beta * D_MODEL ** -0.5),
        'w_ff2': normal(ks[19], (DEPTH, D_FF, D_MODEL), beta * D_FF ** -0.5),
        'ln2_g': gain(ks[20], (DEPTH, D_MODEL)),
        'ln2_b': normal(ks[21], (DEPTH, D_MODEL), 0.02),
    }


def reference(x, positions, ln_in_g, ln_in_b, w_in, w_gla_a2, b_gla_a2, gla_norm_g, w_o_gla,
              q_a_norm_g, w_q_b, kv_a_norm_g, w_kv_b, w_o_mla, b_gate, w_out,
              ln1_g, ln1_b, w_ff1, w_ff2, ln2_g, ln2_b):
    cos, sin = rope_tables(positions)
    h = layer_norm(x, ln_in_g, ln_in_b)
    for l in range(DEPTH):
        mix = token_mixer(h, cos, sin, w_in[l], w_gla_a2[l], b_gla_a2[l], gla_norm_g[l], w_o_gla[l],
                          q_a_norm_g[l], w_q_b[l], kv_a_norm_g[l], w_kv_b[l], w_o_mla[l], b_gate[l], w_out[l])
        h = layer_norm(DEEPNORM_ALPHA * h + mix, ln1_g[l], ln1_b[l])
        ff = jnp.square(jax.nn.relu(h @ w_ff1[l])) @ w_ff2[l]
        h = layer_norm(DEEPNORM_ALPHA * h + ff, ln2_g[l], ln2_b[l])
    return h
```

```python
import numpy as np
from contextlib import ExitStack
import concourse.bass as bass
import concourse.mybir as mybir
from concourse.bass_utils import run_bass_kernel_spmd

F32 = mybir.dt.float32
BF16 = mybir.dt.bfloat16
I32 = mybir.dt.int32
AF = mybir.ActivationFunctionType
ALU = mybir.AluOpType


class Sched:
    ENGS = ('pe', 'act', 'dve', 'pool', 'sp')

    def __init__(self, nc):
        self.nc = nc
        self.ops = {e: [] for e in self.ENGS}
        self.count = {e: 0 for e in self.ENGS}
        self.waited = {e: {} for e in self.ENGS}
        self.last_write = {}
        self.readers = {}
        self.dma_count = {}
        self.final_tokens = []
        self.seq = 0
        self.cut = None
        self.labels = []
        self.bank_last = {}

    def _deps(self, eng, reads, writes):
        deps = {}

        def add(tok):
            if tok is None:
                return
            s, v = tok
            if deps.get(s, 0) < v:
                deps[s] = v
        for k in reads:
            add(self.last_write.get(k))
        for k in writes:
            add(self.last_write.get(k))
            for r in self.readers.get(k, ()):
                add(r)
        for k in list(reads) + list(writes):
            if isinstance(k, tuple) and k[0] == 'ps':
                for e2, tok in self.bank_last.get(k[1], {}).items():
                    if e2 != eng:
                        add(tok)
        waits = []
        for s, v in deps.items():
            if eng == 'pe' and s == 'S_pe':
                continue
            if self.waited[eng].get(s, 0) >= v:
                continue
            self.waited[eng][s] = v
            waits.append((s, v))
        return waits

    def _commit(self, tok, reads, writes, eng=None):
        for k in list(reads) + list(writes):
            if isinstance(k, tuple) and k[0] == 'ps':
                self.bank_last.setdefault(k[1], {})[eng] = tok
        for k in reads:
            self.readers.setdefault(k, []).append(tok)
        for k in writes:
            self.last_write[k] = tok
            self.readers[k] = []

    def op(self, eng, fn, reads=(), writes=()):
        waits = self._deps(eng, reads, writes)
        self.count[eng] += 1
        tok = ('S_' + eng, self.count[eng])
        self.ops[eng].append((waits, fn, tok[0], 1, self.seq))
        self.seq += 1
        self._commit(tok, reads, writes, eng)
        return tok

    def dma(self, eng, out, in_, reads=(), writes=(), sem=None, final=False):
        waits = self._deps(eng, reads, writes)
        sname = 'D_' + sem
        self.dma_count[sname] = self.dma_count.get(sname, 0) + 16
        tok = (sname, self.dma_count[sname])
        self.ops[eng].append((waits, lambda e: e.dma_start(out=out, in_=in_), sname, 16, self.seq))
        self.seq += 1
        self._commit(tok, reads, writes)
        if final:
            self.final_tokens.append(tok + (self.seq - 1,))
        return tok

    def emit(self):
        nc = self.nc
        names = ['S_' + e for e in self.ENGS if e != 'sp'] + sorted(self.dma_count)
        with ExitStack() as st:
            sems = {n: st.enter_context(nc.semaphore(n)) for n in names}
            block = st.enter_context(nc.Block())

            def run(e, h):
                for waits, fn, sname, inc, seq in self.ops[e]:
                    if self.cut is not None and seq >= self.cut:
                        break
                    for s, v in waits:
                        h.wait_ge(sems[s], v)
                    fn(h).then_inc(sems[sname], inc)
                if e == 'sp':
                    fin = {}
                    for e2 in self.ENGS:
                        for waits, fn, sname, inc, seq in self.ops[e2]:
                            if inc == 16 and (self.cut is None or seq < self.cut):
                                fin[sname] = fin.get(sname, 0) + 16
                    for s, v, seq in self.final_tokens:
                        if self.cut is not None and seq >= self.cut:
                            continue
                        fin[s] = max(fin.get(s, 0), v)
                    for s, v in fin.items():
                        h.wait_ge(sems[s], v)

            @block.tensor
            def _(h):
                run('pe', h)

            @block.scalar
            def _(h):
                run('act', h)

            @block.vector
            def _(h):
                run('dve', h)

            @block.gpsimd
            def _(h):
                run('pool', h)

            @block.sync
            def _(h):
                run('sp', h)


P = 128
TG = 512
NG = 4
SEQ = 2048
D = 1024
NSLOT = 4
WELEMS = 4096
NPV = 80
ALPHA = 2.0 ** 0.25
LN_EPS = 1e-5
RMS_EPS = 1e-6
ATT_SCALE = 192.0 ** -0.5
GLA_QSCALE = 128.0 ** -0.5
TWO_PI = 2.0 * np.pi
CW1 = 6.28125
CW2 = TWO_PI - CW1

PV_LNIN_G, PV_LNIN_B, PV_LN1_G, PV_LN1_B, PV_LN2_G, PV_LN2_B = 0, 8, 16, 24, 32, 40
PV_BGATE, PV_GLAG, PV_QG, PV_KVG, PV_INVF, PV_SGN = 48, 64, 72, 75, 77, 78

BLOCKS = [
    ('BA', 8, 512), ('BQ', 8, 400), ('KVK', 2, 1024), ('KVV', 2, 1024),
    ('WQB0', 3, 1024), ('WQB1', 3, 1024),
    ('WOM0', 8, 512), ('BGM0', 8, 512), ('WOM1', 8, 512), ('BGM1', 8, 512),
    ('BGK', 8, 512), ('BGQ', 8, 512), ('BGV0', 8, 512), ('BGV1', 8, 512),
    ('BR0', 8, 512), ('BR1', 8, 512),
    ('WOG0', 8, 512), ('BGG0', 8, 512), ('WOG1', 8, 512), ('BGG1', 8, 512),
    ('WOUT0', 8, 512), ('WOUT1', 8, 512),
    ('FF1_0', 8, 512), ('FF1_1', 8, 512), ('FF1_2', 8, 512), ('FF1_3', 8, 512),
    ('FF2_0', 16, 256), ('FF2_1', 16, 256), ('FF2_2', 16, 256), ('FF2_3', 16, 256),
    ('FF1_4', 8, 512), ('FF1_5', 8, 512), ('FF1_6', 8, 512), ('FF1_7', 8, 512),
    ('FF2_4', 16, 256), ('FF2_5', 16, 256), ('FF2_6', 16, 256), ('FF2_7', 16, 256),
]
BLK_OFF = []
_o = 0
for _n, _kc, _nn in BLOCKS:
    BLK_OFF.append(_o)
    _o += _kc * _nn
WTOT = _o


def build_program(dbg=None, ngroups=NG, cut=None, marks=None):
    nc = bass.Bass("TRN2", target_bir_lowering=False)
    x_d = nc.dram_tensor("x", [SEQ, D], F32, kind="ExternalInput").ap()
    pos_d = nc.dram_tensor("pos", [1, SEQ], I32, kind="ExternalInput").ap()
    ws_d = nc.dram_tensor("wstream", [P, WTOT], F32, kind="ExternalInput").ap()
    pv_d = nc.dram_tensor("pvec", [P, NPV], F32, kind="ExternalInput").ap()
    wa2_d = nc.dram_tensor("wa2", [17, 512], F32, kind="ExternalInput").ap()
    cst_d = nc.dram_tensor("cst", [P, 384], F32, kind="ExternalInput").ap()
    lno_d = nc.dram_tensor("lnout", [2, D], F32, kind="ExternalInput").ap()
    out_d = nc.dram_tensor("out", [SEQ, D], F32, kind="ExternalOutput").ap()
    if dbg:
        dbg_d = nc.dram_tensor("dbg", [P, dbg[1]], F32, kind="ExternalOutput").ap()

    with ExitStack() as st:
        def sb(name, shape, dt):
            return st.enter_context(nc.sbuf_tensor(name, shape, dt))

        KT = sb("KT", [P, 8, SEQ], BF16)
        KR = sb("KR", [P, SEQ], BF16)
        V = sb("V", [P, 16, 1024], BF16)
        Sst = sb("Sst", [P, 4, 256], F32)
        Sbf = sb("Sbf", [P, 4, 256], BF16)
        WS = [sb(f"ws{i}", [P, WELEMS], BF16) for i in range(NSLOT)]
        CST = sb("CST", [P, 384], F32)
        MU = sb("MU", [P, 128], BF16)
        ONES = sb("ONES", [P, 128], BF16)
        PV = sb("PV", [P, NPV], F32)
        WA2 = sb("WA2", [17, 512], F32)
        A17 = sb("A17", [17, 512], F32)
        NBG = sb("NBG", [P, 16], F32)
        G2B2 = sb("G2B2", [P, 2, D], F32)
        LST2 = sb("LST2", [P, 2, 16], F32)
        LST = sb("LST", [P, 2, 16], F32)
        R32 = sb("R32", [P, 8, TG], F32)
        Rb = sb("Rb", [P, 8, TG], BF16)
        XIN = [sb(f"xin{i}", [P, D], F32) for i in range(2)]
        NS = 38
        AR = sb("AR", [P, NS, TG], BF16)
        NT = 8
        TMP = sb("TMP", [P, NT, TG], F32)
        SM = sb("SM", [P, 4, 128], BF16)
        PSB = [st.enter_context(nc.psum_tensor(f"psb{i}", [P, 512], F32)) for i in range(8)]

        POSB = TMP[:, 7, :].bitcast(I32)
        IDENT = CST[:, 0:128]
        UF = CST[:, 128:256]
        LF = CST[:, 256:384]

        S = Sched(nc)
        S.cut = cut

        def psk(b):
            return [('ps', b, q) for q in range(4)]

        def ark(*idx):
            return [('ar', i) for i in idx]

        def tk(*idx):
            return [('tmp', i) for i in idx]

        def pvc(col):
            return PV[:, col:col + 1]

        pools = {'A': [0, 1, 2, 3], 'S': [0, 1, 2, 3], 'B': [4, 5, 6, 7], 'C': [6, 7], 'G': [0, 1, 2, 3, 4, 5, 6, 7], 'F': [0, 1, 2, 3, 4, 5]}
        pool_pos = {k: 0 for k in pools}

        def ps_get(pool):
            b = pools[pool][pool_pos[pool] % len(pools[pool])]
            pool_pos[pool] += 1
            return b

        scr_pos = [0]

        def scr_get():
            i = 36 + (scr_pos[0] % 2)
            scr_pos[0] += 1
            return i

        ln_scr = [16, 17, 18, 19, 36, 37]
        ln_pos = [0]

        def ln_scr_get():
            i = ln_scr[ln_pos[0] % len(ln_scr)]
            ln_pos[0] += 1
            return i

        evac_rr = [0]

        def evac_copy(out_ap, in_ap, reads, writes, eng=None):
            if eng is None:
                eng = 'act' if evac_rr[0] % 2 == 0 else 'dve'
                evac_rr[0] += 1
            if eng == 'act':
                S.op('act', lambda e: e.activation(out_ap, in_ap, AF.Copy), reads, writes)
            else:
                S.op('dve', lambda e: e.tensor_copy(out_ap, in_ap), reads, writes)

        def mm(ps_ap, pairs, reads, writes):
            def fn(e):
                n = len(pairs)
                ins = None
                for i, (l, r) in enumerate(pairs):
                    ins = e.matmul(ps_ap, l, r, start=(i == 0), stop=(i == n - 1))
                return ins
            S.op('pe', fn, reads, writes)

        NB = len(BLOCKS)
        total_blocks = ngroups * NB
        wstate = {'next_dma': 0, 'next_use': 0}
        released = [False] * total_blocks

        def w_issue():
            while wstate['next_dma'] < total_blocks and (
                    wstate['next_dma'] < NSLOT or released[wstate['next_dma'] - NSLOT]):
                i = wstate['next_dma']
                name, kc, n = BLOCKS[i % NB]
                off = BLK_OFF[i % NB]
                s = i % NSLOT
                S.dma('pool', WS[s][:, 0:kc * n], ws_d[:, off:off + kc * n],
                      reads=([('xin', 0), ('xin', 1), 'cst', 'pv'] if i == 0 else []), writes=[('w', s)], sem=f'w{s}')
                wstate['next_dma'] += 1

        def wget(name):
            i = wstate['next_use']
            bname, kc, n = BLOCKS[i % NB]
            assert bname == name, (bname, name)
            wstate['next_use'] += 1
            s = i % NSLOT
            view = WS[s][:, 0:kc * n].rearrange("p (k n) -> p k n", k=kc)
            return i, view, ('w', s)

        def wrel(i):
            released[i] = True
            w_issue()

        for tt0 in range(2):
            S.dma('sp', XIN[tt0][:, :], x_d[tt0 * P:(tt0 + 1) * P, :], writes=[('xin', tt0)], sem=f'xin{tt0}')
        S.dma('sp', CST[:, :], cst_d[:, :], writes=['cst'], sem='cst')
        S.dma('sp', PV[:, :], pv_d[:, :], writes=['pv'], sem='pv')
        S.dma('sp', WA2[:, :], wa2_d[:, :], writes=['wa2'], sem='wa2')
        S.dma('sp', G2B2[:, 0, :], lno_d[0:1, :].partition_broadcast(P), writes=['g2'], sem='g2')
        S.dma('sp', G2B2[:, 1, :], lno_d[1:2, :].partition_broadcast(P), writes=['b2'], sem='b2')
        S.op('dve', lambda e: e.memset(ONES[:, :], 1.0), writes=['ones'])
        S.op('dve', lambda e: e.memset(A17[:, :], 1.0), writes=['a17'])
        S.op('dve', lambda e: e.memset(Sst[:, :, :], 0.0), writes=[('Sst', h) for h in range(4)])
        S.op('dve', lambda e: e.memset(Sbf[:, :, :], 0.0), writes=[('Sbf', h) for h in range(4)])
        S.op('dve', lambda e: e.tensor_copy(MU[:, :], UF), reads=['cst'], writes=['mu'])
        S.op('dve', lambda e: e.tensor_scalar(MU[:, :], MU[:, :], 16.0, None, ALU.mult), reads=['mu'], writes=['mu'])
        S.op('dve', lambda e: e.tensor_scalar(NBG[:, :], PV[:, PV_BGATE:PV_BGATE + 16], -1.0, None, ALU.mult), reads=['pv'], writes=['nbg'])
        w_issue()

        def layernorm(gcol, bcol, want_bf16=True):
            b_sum, b_sq = 6, 7
            for c in range(8):
                s1, s2 = ln_scr_get(), ln_scr_get()
                S.op('act', lambda e, c=c, s1=s1: e.activation(AR[:, s1, :], R32[:, c, :], AF.Copy),
                     reads=[('R32', c)], writes=ark(s1))
                S.op('act', lambda e, c=c, s2=s2: e.activation(AR[:, s2, :], R32[:, c, :], AF.Square),
                     reads=[('R32', c)], writes=ark(s2))
                S.op('pe', lambda e, c=c, s1=s1: e.matmul(PSB[b_sum][:, :], ONES[:, :], AR[:, s1, :], start=(c == 0), stop=(c == 7)),
                     reads=ark(s1) + ['ones'] + psk(b_sum), writes=psk(b_sum))
                S.op('pe', lambda e, c=c, s2=s2: e.matmul(PSB[b_sq][:, :], ONES[:, :], AR[:, s2, :], start=(c == 0), stop=(c == 7)),
                     reads=ark(s2) + ['ones'] + psk(b_sq), writes=psk(b_sq))
            T_mean, T_r, T_m2 = TMP[:, 0, :], TMP[:, 1, :], TMP[:, 2, :]
            S.op('dve', lambda e: e.tensor_scalar(T_mean, PSB[b_sum][:, :], 1.0 / D, None, ALU.mult),
                 reads=psk(b_sum), writes=tk(0))
            S.op('dve', lambda e: e.tensor_tensor(T_m2, T_mean, T_mean, ALU.mult), reads=tk(0), writes=tk(2))
            S.op('dve', lambda e: e.scalar_tensor_tensor(T_r, PSB[b_sq][:, :], 1.0 / D, T_m2, ALU.mult, ALU.subtract),
                 reads=psk(b_sq) + tk(2), writes=tk(1))
            S.op('act', lambda e: e.activation(T_r, T_r, AF.Ln, bias=LN_EPS, scale=1.0), reads=tk(1), writes=tk(1))
            S.op('act', lambda e: e.activation(T_r, T_r, AF.Exp, scale=-0.5), reads=tk(1), writes=tk(1))
            S.op('dve', lambda e: e.scalar_tensor_tensor(T_m2, T_mean, -1.0, T_r, ALU.mult, ALU.mult), reads=tk(0, 1), writes=tk(2))
            for c in range(8):
                S.op('dve', lambda e, c=c: e.tensor_tensor(R32[:, c, :], R32[:, c, :], T_r, ALU.mult),
                     reads=[('R32', c)] + tk(1), writes=[('R32', c)])
                S.op('dve', lambda e, c=c: e.tensor_tensor(R32[:, c, :], R32[:, c, :], T_m2, ALU.add),
                     reads=[('R32', c)] + tk(2), writes=[('R32', c)])
                if want_bf16:
                    S.op('act', lambda e, c=c: e.activation(Rb[:, c, :], R32[:, c, :], AF.Identity, bias=pvc(bcol + c), scale=pvc(gcol + c)),
                         reads=[('R32', c), 'pv'], writes=[('Rb', c)])
            for c in range(8):
                S.op('act', lambda e, c=c: e.activation(R32[:, c, :], R32[:, c, :], AF.Identity, bias=pvc(bcol + c), scale=pvc(gcol + c)),
                     reads=[('R32', c), 'pv'], writes=[('R32', c)])

        def proj(bank, wv, wkey, col0, m, rhs_aps, rhs_keys, prow=P):
            pairs = [(wv[:, kc, col0:col0 + m], rhs_aps[kc]) for kc in range(len(rhs_aps))]
            mm(PSB[bank][0:m, :], pairs, reads=[wkey] + rhs_keys, writes=psk(bank))

        def proj_kc_major(banks, wv, wkey, col0s, m=128):
            for kc in range(8):
                for bank, col0 in zip(banks, col0s):
                    S.op('pe', lambda e, bank=bank, col0=col0, kc=kc: e.matmul(PSB[bank][0:m, :], wv[:, kc, col0:col0 + m], Rb[:, kc, :],
                                                                              start=(kc == 0), stop=(kc == 7)),
                         reads=[wkey, ('Rb', kc)] + psk(bank), writes=psk(bank))

        def x_chain(xs):
            lk = [('lst', xs)]
            S.op('dve', lambda e: e.bn_stats(LST[:, xs, 0:6], XIN[xs][:, 0:512]), reads=[('xin', xs)], writes=lk)
            S.op('dve', lambda e: e.bn_stats(LST[:, xs, 6:12], XIN[xs][:, 512:1024]), reads=[('xin', xs)] + lk, writes=lk)
            S.op('dve', lambda e: e.bn_aggr(LST[:, xs, 12:14], LST[:, xs, 0:12]), reads=lk, writes=lk)
            S.op('act', lambda e: e.activation(LST[:, xs, 14:15], LST[:, xs, 13:14], AF.Ln, bias=LN_EPS, scale=1.0), reads=lk, writes=lk)
            S.op('act', lambda e: e.activation(LST[:, xs, 14:15], LST[:, xs, 14:15], AF.Exp, scale=-0.5), reads=lk, writes=lk)
            S.op('dve', lambda e: e.scalar_tensor_tensor(LST[:, xs, 15:16], LST[:, xs, 12:13], -1.0, LST[:, xs, 14:15], ALU.mult, ALU.mult), reads=lk, writes=lk)
            S.op('act', lambda e: e.activation(XIN[xs][:, :], XIN[xs][:, :], AF.Identity, bias=LST[:, xs, 15:16], scale=LST[:, xs, 14:15]),
                 reads=[('xin', xs)] + lk, writes=[('xin', xs)])

        def rb_aps():
            return [Rb[:, kc, :] for kc in range(8)]

        def rb_keys():
            return [('Rb', kc) for kc in range(8)]

        def rms_rstd(ps_banks, nfeat, t_out):
            b_stat = 7
            n = len(ps_banks)
            for i, b in enumerate(ps_banks):
                s2 = scr_get()
                S.op('act', lambda e, b=b, s2=s2: e.activation(AR[:, s2, :], PSB[b][:, :], AF.Square),
                     reads=psk(b), writes=ark(s2))
                S.op('pe', lambda e, i=i, s2=s2: e.matmul(PSB[b_stat][:, :], ONES[:, :], AR[:, s2, :], start=(i == 0), stop=(i == n - 1)),
                     reads=ark(s2) + ['ones'] + psk(b_stat), writes=psk(b_stat))
            T = TMP[:, t_out, :]
            S.op('act', lambda e: e.activation(T, PSB[b_stat][:, :], AF.Ln, bias=RMS_EPS, scale=1.0 / nfeat),
                 reads=psk(b_stat), writes=tk(t_out))
            S.op('act', lambda e: e.activation(T, T, AF.Exp, scale=-0.5), reads=tk(t_out), writes=tk(t_out))

        def do_group(g):
            t0 = g * TG
            S.labels.append(('1', S.seq))
            for tt in range(4):
                xs = (g * 4 + tt) % 2
                if tt >= 2:
                    S.dma('sp', XIN[xs][:, :], x_d[t0 + tt * P:t0 + (tt + 1) * P, :], writes=[('xin', xs)], sem=f'xin{xs}')
                if g == 0 or tt >= 2:
                    x_chain(xs)
                for half in range(2):
                    b = ps_get('A')

                    def fn(e, xs=xs, half=half, b=b):
                        ins = None
                        for q in range(4):
                            c = half * 4 + q
                            ins = e.transpose(PSB[b][:, q * P:(q + 1) * P], XIN[xs][:, c * P:(c + 1) * P], IDENT)
                        return ins
                    S.op('pe', fn, reads=[('xin', xs), 'cst'], writes=psk(b))
                    evac_copy(R32[:, half * 4:half * 4 + 4, tt * P:(tt + 1) * P],
                              PSB[b][:, :].rearrange("p (a b) -> p a b", a=4),
                              reads=psk(b), writes=[('R32', half * 4 + q) for q in range(4)])
                if tt == 1:
                    yield 'headA'
            S.labels.append(('2', S.seq))
            for c in range(8):
                if c % 2 == 0:
                    S.op('act', lambda e, c=c: e.activation(Rb[:, c, :], R32[:, c, :], AF.Identity, bias=pvc(PV_LNIN_B + c), scale=pvc(PV_LNIN_G + c)),
                         reads=[('R32', c), 'pv'], writes=[('Rb', c)])
                else:
                    S.op('dve', lambda e, c=c: e.tensor_scalar(Rb[:, c, :], R32[:, c, :], pvc(PV_LNIN_G + c), pvc(PV_LNIN_B + c), ALU.mult, ALU.add),
                         reads=[('R32', c), 'pv'], writes=[('Rb', c)])

            def deferred_r32_affine():
                for c in range(8):
                    S.op('dve', lambda e, c=c: e.tensor_scalar(R32[:, c, :], R32[:, c, :], pvc(PV_LNIN_G + c), pvc(PV_LNIN_B + c), ALU.mult, ALU.add),
                         reads=[('R32', c), 'pv'], writes=[('R32', c)])

            S.labels.append(('0', S.seq))
            S.dma('sp', POSB, pos_d[0:1, t0:t0 + TG].partition_broadcast(P), writes=tk(7), sem='pos')
            T_ang, T_k, T_c, T_s = TMP[:, 5, :], TMP[:, 6, :], TMP[:, 3, :], TMP[:, 4, :]
            KI = POSB
            S.op('dve', lambda e: e.tensor_copy(T_ang, POSB), reads=tk(7), writes=tk(5))
            S.op('dve', lambda e: e.tensor_scalar(T_ang, T_ang, pvc(PV_INVF), None, ALU.mult), reads=tk(5) + ['pv'], writes=tk(5))
            S.op('dve', lambda e: e.tensor_scalar(T_k, T_ang, 1.0 / TWO_PI, None, ALU.mult), reads=tk(5), writes=tk(6))
            S.op('dve', lambda e: e.tensor_copy(KI, T_k), reads=tk(6), writes=tk(7))
            S.op('dve', lambda e: e.tensor_copy(T_k, KI), reads=tk(7), writes=tk(6))
            S.op('dve', lambda e: e.scalar_tensor_tensor(T_ang, T_k, -CW1, T_ang, ALU.mult, ALU.add), reads=tk(5, 6), writes=tk(5))
            S.op('dve', lambda e: e.scalar_tensor_tensor(T_ang, T_k, -CW2, T_ang, ALU.mult, ALU.add), reads=tk(5, 6), writes=tk(5))
            S.op('dve', lambda e: e.tensor_scalar(T_ang, T_ang, float(np.pi), float(-np.pi), ALU.min, ALU.max), reads=tk(5), writes=tk(5))
            S.op('act', lambda e: e.activation(T_s, T_ang, AF.Sin), reads=tk(5), writes=tk(4))
            S.op('dve', lambda e: e.tensor_scalar(T_s, T_s, pvc(PV_SGN), None, ALU.mult), reads=tk(4) + ['pv'], writes=tk(4))
            S.op('dve', lambda e: e.scalar_tensor_tensor(T_k, T_ang, -1.0, T_ang, ALU.mult, ALU.max), reads=tk(5), writes=tk(6))
            S.op('act', lambda e: e.activation(T_c, T_k, AF.Sin, bias=float(np.pi / 2), scale=-1.0), reads=tk(6), writes=tk(3))

            S.labels.append(('3', S.seq))
            iBA, wBA, kBA = wget('BA')
            b_kr, b_krs = ps_get('A'), ps_get('A')
            b_c = [ps_get('A'), ps_get('A')]
            proj_kc_major([b_kr, b_krs, b_c[0], b_c[1]], wBA, kBA, [0, 128, 256, 384])
            S.op('dve', lambda e: e.tensor_tensor(TMP[:, 5, :], PSB[b_kr][:, :], T_c, ALU.mult), reads=psk(b_kr) + tk(3), writes=tk(5))
            S.op('dve', lambda e: e.tensor_tensor(TMP[:, 6, :], PSB[b_krs][:, :], T_s, ALU.mult), reads=psk(b_krs) + tk(4), writes=tk(6))
            S.op('dve', lambda e: e.tensor_tensor(KR[:, t0:t0 + TG], TMP[:, 5, :], TMP[:, 6, :], ALU.add), reads=tk(5, 6), writes=[('KR', g)])
            wrel(iBA)
            CK32 = [XIN[0][:, 0:512], XIN[0][:, 512:1024]]
            QL32 = [XIN[1][:, 0:512], XIN[1][:, 512:1024], TMP[:, 7, :]]
            QLK = [[('xin', 1)], [('xin', 1)], tk(7)]
            for i in range(2):
                evac_copy(CK32[i], PSB[b_c[i]][:, :], reads=psk(b_c[i]), writes=[('xin', 0)], eng='act')
            rms_rstd(b_c, 256, 1)
            for i in range(2):
                S.op('dve', lambda e, i=i: e.scalar_tensor_tensor(AR[:, i, :], CK32[i], pvc(PV_KVG + i), TMP[:, 1, :], ALU.mult, ALU.mult),
                     reads=tk(1) + [('xin', 0), 'pv'], writes=ark(i))
            iBQ, wBQ, kBQ = wget('BQ')
            b_q = [ps_get('A'), ps_get('A'), ps_get('A')]
            for i in range(3):
                proj(b_q[i], wBQ, kBQ, i * 128, 128, rb_aps(), rb_keys())
            for i in range(3):
                evac_copy(QL32[i], PSB[b_q[i]][:, :], reads=psk(b_q[i]), writes=QLK[i], eng='act')
            rms_rstd(b_q, 384, 1)
            for i in range(3):
                S.op('dve', lambda e, i=i: e.scalar_tensor_tensor(AR[:, 2 + i, :], QL32[i], pvc(PV_QG + i), TMP[:, 1, :], ALU.mult, ALU.mult),
                     reads=tk(1) + QLK[i] + ['pv'], writes=ark(2 + i))
            b_a = ps_get('A')
            proj(b_a, wBQ, kBQ, 384, 16, rb_aps(), rb_keys())
            wrel(iBQ)
            S.op('dve', lambda e: e.tensor_copy(A17[0:16, :], PSB[b_a][0:16, :]), reads=psk(b_a) + ['a17'], writes=['a17'])
            ck_aps = [AR[:, 0, :], AR[:, 1, :]]
            ck_keys = ark(0, 1)
            iW, wW, kW = wget('KVK')
            for h in range(8):
                b = ps_get('A')
                proj(b, wW, kW, h * 128, 128, ck_aps, ck_keys)
                evac_copy(KT[:, h, t0:t0 + TG], PSB[b][:, :], reads=psk(b), writes=[('KT', h, g)], eng='act')
            wrel(iW)
            iW, wW, kW = wget('KVV')
            for tt in range(4):
                for half in range(2):
                    b = ps_get('A')
                    pairs = [(AR[:, kc, tt * P:(tt + 1) * P], wW[:, kc, half * 512:(half + 1) * 512]) for kc in range(2)]
                    mm(PSB[b][:, :], pairs, reads=[kW] + ck_keys, writes=psk(b))
                    evac_copy(V[:, g * 4 + tt, half * 512:(half + 1) * 512], PSB[b][:, :], reads=psk(b), writes=[('V', g * 4 + tt, half)], eng=('act' if half == 0 else None))
            wrel(iW)

            QR = [13, 14, 15, 16, 0, 1, 36, 37]
            qn_aps = [AR[:, 2 + i, :] for i in range(3)]
            qn_keys = ark(2, 3, 4)
            for hb in range(2):
                iW, wW, kW = wget(f'WQB{hb}')
                for hl in range(4):
                    h = hb * 4 + hl
                    b = ps_get('A')
                    proj(b, wW, kW, hl * 128, 128, qn_aps, qn_keys)
                    evac_copy(AR[:, 5 + h, :], PSB[b][:, :], reads=psk(b), writes=ark(5 + h), eng='act')
                for pl in range(2):
                    pr = hb * 2 + pl
                    b_r, b_s = ps_get('A'), ps_get('A')
                    proj(b_r, wW, kW, 512 + pl * 128, 128, qn_aps, qn_keys)
                    proj(b_s, wW, kW, 768 + pl * 128, 128, qn_aps, qn_keys)
                    S.op('dve', lambda e, b_r=b_r: e.tensor_tensor(TMP[:, 5, :], PSB[b_r][:, :], T_c, ALU.mult), reads=psk(b_r) + tk(3), writes=tk(5))
                    S.op('dve', lambda e, b_s=b_s: e.tensor_tensor(TMP[:, 6, :], PSB[b_s][:, :], T_s, ALU.mult), reads=psk(b_s) + tk(4), writes=tk(6))
                    sa, sb_ = QR[2 * pr], QR[2 * pr + 1]
                    S.op('dve', lambda e, sa=sa: e.tensor_tensor(AR[0:64, sa, :], TMP[0:64, 5, :], TMP[0:64, 6, :], ALU.add), reads=tk(5, 6), writes=ark(sa))
                    S.op('act', lambda e, sa=sa: e.memzero(AR[64:128, sa, :]), writes=ark(sa))
                    S.op('dve', lambda e, sb_=sb_: e.tensor_tensor(AR[64:128, sb_, :], TMP[64:128, 5, :], TMP[64:128, 6, :], ALU.add), reads=tk(5, 6), writes=ark(sb_))
                    S.op('act', lambda e, sb_=sb_: e.memzero(AR[0:64, sb_, :]), writes=ark(sb_))
                wrel(iW)
            S.labels.append(('4', S.seq))
            blocks = [(h, kt) for h in range(8) for kt in range(4 * g + 4)]
            nkt = 4 * g + 4

            def score_block(h, kt):
                r = kt - 4 * g
                q0 = P * max(r, 0)
                b = ps_get('S')
                qs = QR[h]

                def fn(e):
                    e.matmul(PSB[b][:, q0:TG], KT[:, h, kt * P:(kt + 1) * P], AR[:, 5 + h, q0:TG], start=True, stop=False)
                    return e.matmul(PSB[b][:, q0:TG], KR[:, kt * P:(kt + 1) * P], AR[:, qs, q0:TG], start=False, stop=True)
                S.op('pe', fn, reads=[('KT', h, kt // 4), ('KR', kt // 4)] + ark(5 + h, qs), writes=psk(b))
                return b, q0, r

            deferred = []
            pendq = [score_block(*blocks[0])]
            for bj in (1, 2):
                if len(blocks) > bj:
                    pendq.append(score_block(*blocks[bj]))
            pt_pos = 0
            for bi, (h, kt) in enumerate(blocks):
                b, q0, r = pendq.pop(0)
                if bi + 3 < len(blocks):
                    pendq.append(score_block(*blocks[bi + 3]))
                pt = 17 + (pt_pos % 3)
                pt_pos += 1
                S.op('act', lambda e, b=b, q0=q0, pt=pt: e.activation(AR[:, pt, q0:TG], PSB[b][:, q0:TG], AF.Exp, scale=ATT_SCALE),
                     reads=psk(b), writes=ark(pt))
                if r >= 0:
                    S.op('dve', lambda e, q0=q0, pt=pt: e.tensor_tensor(AR[:, pt, q0:q0 + P], AR[:, pt, q0:q0 + P], MU[:, :], ALU.mult),
                         reads=ark(pt) + ['mu'], writes=ark(pt))
                b_o = 4 + (h % 2)
                b_s = 6 + (h % 2)
                S.op('pe', lambda e, h=h, kt=kt, q0=q0, pt=pt, b_o=b_o: e.matmul(
                    PSB[b_o][:, q0:TG], V[:, kt, h * P:(h + 1) * P], AR[:, pt, q0:TG], start=(kt == 0), stop=(kt == nkt - 1)),
                    reads=ark(pt) + [('V', kt, h // 4)] + psk(b_o), writes=psk(b_o))
                S.op('pe', lambda e, kt=kt, q0=q0, pt=pt, b_s=b_s: e.matmul(
                    PSB[b_s][:, q0:TG], ONES[:, :], AR[:, pt, q0:TG], start=(kt == 0), stop=(kt == nkt - 1)),
                    reads=ark(pt) + ['ones'] + psk(b_s), writes=psk(b_s))
                for fdef in deferred:
                    fdef()
                deferred.clear()
                if bi == 1:
                    deferred_r32_affine()
                if kt == nkt - 1:
                    def head_end(h=h, b_o=b_o, b_s=b_s):
                        tr = h % 2
                        S.op('act', lambda e: e.activation(TMP[:, tr, :], PSB[b_s][:, :], AF.Ln), reads=psk(b_s), writes=tk(tr))
                        S.op('act', lambda e: e.activation(TMP[:, tr, :], TMP[:, tr, :], AF.Exp, scale=-1.0), reads=tk(tr), writes=tk(tr))
                        S.op('dve', lambda e: e.tensor_tensor(AR[:, 20 + h, :], PSB[b_o][:, :], TMP[:, tr, :], ALU.mult),
                             reads=psk(b_o) + tk(tr), writes=ark(20 + h))
                    deferred.append(head_end)
            for fdef in deferred:
                fdef()
            deferred.clear()

            S.labels.append(('5', S.seq))
            def out_proj_gated(wname, gname, x_slots, gate_col0, add_prev):
                for hb in range(2):
                    iW, wW, kW = wget(f'{wname}{hb}')
                    iG, wG, kG = wget(f'{gname}{hb}')
                    for cl in range(4):
                        c = hb * 4 + cl
                        b_g, b_y = ps_get('G'), ps_get('G')
                        proj(b_g, wG, kG, cl * 128, 128, rb_aps(), rb_keys())
                        proj(b_y, wW, kW, cl * 128, 128, [AR[:, s, :] for s in x_slots], ark(*x_slots))
                        tg_ = 2 + (c % 2)
                        S.op('act', lambda e, b_g=b_g, tg_=tg_, c=c: e.activation(TMP[:, tg_, :], PSB[b_g][:, :], AF.Sigmoid, bias=pvc(PV_BGATE + gate_col0 + c), scale=1.0),
                             reads=psk(b_g) + ['pv'], writes=tk(tg_))
                        if not add_prev:
                            S.op('dve', lambda e, b_y=b_y, tg_=tg_, c=c: e.tensor_tensor(AR[:, 28 + c, :], PSB[b_y][:, :], TMP[:, tg_, :], ALU.mult),
                                 reads=psk(b_y) + tk(tg_), writes=ark(28 + c))
                        else:
                            S.op('dve', lambda e, b_y=b_y, tg_=tg_: e.tensor_tensor(TMP[:, tg_, :], PSB[b_y][:, :], TMP[:, tg_, :], ALU.mult),
                                 reads=psk(b_y) + tk(tg_), writes=tk(tg_))
                            S.op('dve', lambda e, tg_=tg_, c=c: e.tensor_tensor(AR[:, 28 + c, :], TMP[:, tg_, :], AR[:, 28 + c, :], ALU.add),
                                 reads=tk(tg_) + ark(28 + c), writes=ark(28 + c))
                    wrel(iW)
                    wrel(iG)

            out_proj_gated('WOM', 'BGM', list(range(20, 28)), 8, add_prev=False)

            S.labels.append(('6', S.seq))
            iK, wK, kK = wget('BGK')
            EB = lambda h: TMP[:, 4 + h, :]
            ENB = lambda h: XIN[h // 2][:, (h % 2) * 512:(h % 2 + 1) * 512]
            b_zs = [ps_get('G') for _ in range(4)]
            for tt in range(4):
                S.op('pe', lambda e, b_z=b_zs[tt], tt=tt: e.matmul(PSB[b_z][:, :], A17[0:17, tt * P:(tt + 1) * P], WA2[0:17, :], start=True, stop=True),
                     reads=['a17', 'wa2'], writes=psk(b_zs[tt]))

            def gla_chain(tt):
                ta, la = 2 * (tt % 2), 2 * (tt % 2) + 1
                b_z = b_zs[tt]
                S.op('act', lambda e: e.activation(TMP[:, ta, :], PSB[b_z][:, :], AF.Abs), reads=psk(b_z), writes=tk(ta))
                S.op('act', lambda e: e.activation(TMP[:, ta, :], TMP[:, ta, :], AF.Exp, scale=-1.0), reads=tk(ta), writes=tk(ta))
                S.op('act', lambda e: e.activation(TMP[:, ta, :], TMP[:, ta, :], AF.Ln, bias=1.0, scale=1.0), reads=tk(ta), writes=tk(ta))
                S.op('dve', lambda e: e.scalar_tensor_tensor(TMP[:, la, :], PSB[b_z][:, :], 0.0, TMP[:, ta, :], ALU.min, ALU.subtract),
                     reads=psk(b_z) + tk(ta), writes=tk(la))

            def gla_cum(tt):
                tsl = slice(tt * P, (tt + 1) * P)
                la = 2 * (tt % 2) + 1
                LA = TMP[:, la, :]
                b_k = ps_get('G')
                mm(PSB[b_k][:, :], [(Rb[:, kc, tsl], wK[:, kc, :]) for kc in range(8)], reads=[kK] + rb_keys(), writes=psk(b_k))
                b_b = ps_get('G')

                def fnb(e):
                    ins = None
                    for h in range(4):
                        ins = e.matmul(PSB[b_b][:, h * P:(h + 1) * P], LA[:, h * P:(h + 1) * P], UF, start=True, stop=True)
                    return ins
                S.op('pe', fnb, reads=tk(la) + ['cst'], writes=psk(b_b))
                b_r = ps_get('G')
                S.op('pe', lambda e: e.matmul(PSB[b_r][:, :], LF, LA, start=True, stop=True), reads=tk(la) + ['cst'], writes=psk(b_r))
                S.op('act', lambda e: e.activation(TMP[:, 4:8, tsl], PSB[b_b][:, :].rearrange("p (a b) -> p a b", a=4), AF.Exp),
                     reads=psk(b_b), writes=tk(4, 5, 6, 7))
                for jx in range(2):
                    S.op('act', lambda e, jx=jx: e.activation(
                        XIN[jx][:, :].rearrange("p (a b) -> p a b", a=2)[:, :, tsl],
                        PSB[b_b][:, jx * 256:(jx + 1) * 256].rearrange("p (a b) -> p a b", a=2), AF.Exp, scale=-1.0),
                        reads=psk(b_b), writes=[('xin', jx)])
                S.op('act', lambda e: e.activation(LA, PSB[b_r][:, :], AF.Exp), reads=psk(b_r), writes=tk(la))
                S.op('dve', lambda e: e.tensor_tensor(AR[:, 8 + tt, :], PSB[b_k][:, :], LA, ALU.mult),
                     reads=psk(b_k) + tk(la), writes=ark(8 + tt))

            gla_chain(0)
            gla_chain(1)
            gla_cum(0)
            gla_chain(2)
            gla_cum(1)
            gla_chain(3)
            gla_cum(2)
            gla_cum(3)
            for h in range(4):
                b = ps_get('A')
                proj(b, wK, kK, h * 128, 128, rb_aps(), rb_keys())
                S.op('dve', lambda e, b=b, h=h: e.tensor_tensor(AR[:, 4 + h, :], PSB[b][:, :], ENB(h), ALU.mult),
                     reads=psk(b) + [('xin', h // 2)], writes=ark(4 + h))
            wrel(iK)
            iQ, wQ, kQ = wget('BGQ')
            for h in range(4):
                b = ps_get('A')
                proj(b, wQ, kQ, h * 128, 128, rb_aps(), rb_keys())
                S.op('dve', lambda e, b=b, h=h: e.scalar_tensor_tensor(AR[:, h, :], PSB[b][:, :], GLA_QSCALE, EB(h), ALU.mult, ALU.mult),
                     reads=psk(b) + tk(4 + h), writes=ark(h))
            wrel(iQ)
            if g + 1 < ngroups:
                for tt in range(2):
                    S.dma('sp', XIN[tt][:, :], x_d[t0 + TG + tt * P:t0 + TG + (tt + 1) * P, :], writes=[('xin', tt)], sem=f'xin{tt}')
            for blk in range(2):
                iV, wV, kV = wget(f'BGV{blk}')
                for tt in range(4):
                    b = ps_get('A')
                    mm(PSB[b][:, :], [(Rb[:, kc, tt * P:(tt + 1) * P], wV[:, kc, :]) for kc in range(8)], reads=[kV] + rb_keys(), writes=psk(b))
                    evac_copy(AR[:, 12 + 2 * tt + blk, :], PSB[b][:, :], reads=psk(b), writes=ark(12 + 2 * tt + blk))
                wrel(iV)

            sm_pos = 0
            kv_pos = 0
            for pair in range(2):
                steps = [(tt, hl) for tt in range(4) for hl in range(2)]

                def emit_sT(tt, hl, pair=pair):
                    nonlocal sm_pos
                    h = pair * 2 + hl
                    tsl = slice(tt * P, (tt + 1) * P)
                    qd = sm_pos % 4
                    bsT = 4 + 2 * (sm_pos % 2)
                    sm_pos += 1
                    S.op('pe', lambda e: e.matmul(PSB[bsT][:, 0:P], AR[:, 4 + h, tsl], AR[:, h, tsl], start=True, stop=True),
                         reads=ark(4 + h, h), writes=psk(bsT))
                    S.op('dve', lambda e: e.tensor_tensor(SM[:, qd, :], PSB[bsT][:, 0:P], MU[:, :], ALU.mult),
                         reads=psk(bsT) + ['mu'], writes=[('sm', qd)])
                    return qd

                pend_qd = emit_sT(*steps[0])
                for si, (tt, hl) in enumerate(steps):
                    qd = pend_qd
                    if si + 1 < len(steps):
                        pend_qd = emit_sT(*steps[si + 1])
                    tsl = slice(tt * P, (tt + 1) * P)
                    h = pair * 2 + hl
                    vslot = 12 + 2 * tt + (h // 2)
                    vc0 = (h % 2) * 256
                    for dvc in range(2):
                        b_o = hl * 2 + dvc

                        def fno(e, h=h, tsl=tsl, qd=qd, dvc=dvc, b_o=b_o, vslot=vslot, vc0=vc0, tt=tt):
                            e.matmul(PSB[b_o][:, tsl], AR[:, vslot, vc0 + dvc * P:vc0 + (dvc + 1) * P], SM[:, qd, :], start=True, stop=False)
                            return e.matmul(PSB[b_o][:, tsl], Sbf[:, h, dvc * P:(dvc + 1) * P], AR[:, h, tsl], start=False, stop=True)
                        S.op('pe', fno, reads=ark(vslot, h) + [('sm', qd), ('Sbf', h)], writes=[('ps', b_o, tt)])
                    bkv = 5 + 2 * (kv_pos % 2)
                    kv_pos += 1
                    S.op('pe', lambda e, h=h, tt=tt, bkv=bkv, vslot=vslot, vc0=vc0: e.matmul(
                        PSB[bkv][:, 0:256], AR[:, 8 + tt, h * P:(h + 1) * P], AR[:, vslot, vc0:vc0 + 256], start=True, stop=True),
                        reads=ark(8 + tt, vslot), writes=psk(bkv))
                    S.op('dve', lambda e, h=h, tt=tt, bkv=bkv: e.scalar_tensor_tensor(
                        Sst[:, h, :], Sst[:, h, :], TMP[:, 4 + h, tt * P + P - 1:tt * P + P], PSB[bkv][:, 0:256], ALU.mult, ALU.add),
                        reads=[('Sst', h)] + psk(bkv) + tk(4 + h), writes=[('Sst', h)])
                    S.op('act', lambda e, h=h: e.activation(Sbf[:, h, :], Sst[:, h, :], AF.Copy), reads=[('Sst', h)], writes=[('Sbf', h)])
                iR, wR, kR = wget(f'BR{pair}')
                b_stat = 7
                rst = [1, 0]
                chunks = [(hl, dvc) for hl in range(2) for dvc in range(2)]

                def r_proj(ci):
                    hl, dvc = chunks[ci]
                    c = 2 * (pair * 2 + hl) + dvc
                    b_rr = 4 + (ci % 3)
                    proj(b_rr, wR, kR, (c % 4) * 128, 128, rb_aps(), rb_keys())
                    return b_rr
                brs = []
                for hl in range(2):
                    banks = [hl * 2, hl * 2 + 1]
                    sqs = []
                    for i, bq in enumerate(banks):
                        s2 = scr_get()
                        sqs.append(s2)
                        S.op('act', lambda e, bq=bq, s2=s2: e.activation(AR[:, s2, :], PSB[bq][:, :], AF.Square), reads=psk(bq), writes=ark(s2))
                    brs.append(r_proj(hl))
                    for i, s2 in enumerate(sqs):
                        S.op('pe', lambda e, i=i, s2=s2: e.matmul(PSB[b_stat][:, :], ONES[:, :], AR[:, s2, :], start=(i == 0), stop=(i == 1)),
                             reads=ark(s2) + ['ones'] + psk(b_stat), writes=psk(b_stat))
                    tr_ = rst[hl]
                    S.op('act', lambda e, tr_=tr_: e.activation(TMP[:, tr_, :], PSB[b_stat][:, :], AF.Ln, bias=RMS_EPS, scale=1.0 / 256), reads=psk(b_stat), writes=tk(tr_))
                    S.op('act', lambda e, tr_=tr_: e.activation(TMP[:, tr_, :], TMP[:, tr_, :], AF.Exp, scale=-0.5), reads=tk(tr_), writes=tk(tr_))
                brs.append(r_proj(2))
                for ci, (hl, dvc) in enumerate(chunks):
                    h = pair * 2 + hl
                    c = 2 * h + dvc
                    b_o = hl * 2 + dvc
                    tr_ = rst[hl]
                    ts_ = 2 + (c % 2)
                    if ci == 3:
                        brs.append(r_proj(3))
                    b_rr = brs[ci]
                    S.op('act', lambda e, b_rr=b_rr, ts_=ts_: e.activation(TMP[:, ts_, :], PSB[b_rr][:, :], AF.Silu), reads=psk(b_rr), writes=tk(ts_))
                    S.op('dve', lambda e, ts_=ts_, tr_=tr_: e.tensor_tensor(TMP[:, ts_, :], TMP[:, ts_, :], TMP[:, tr_, :], ALU.mult), reads=tk(ts_, tr_), writes=tk(ts_))
                    S.op('dve', lambda e, c=c, b_o=b_o, ts_=ts_: e.scalar_tensor_tensor(AR[:, 20 + c, :], PSB[b_o][:, :], pvc(PV_GLAG + c), TMP[:, ts_, :], ALU.mult, ALU.mult),
                         reads=psk(b_o) + tk(ts_) + ['pv'], writes=ark(20 + c))
                wrel(iR)

            out_proj_gated('WOG', 'BGG', list(range(20, 28)), 0, add_prev=True)

            S.labels.append(('7', S.seq))
            for hb in range(2):
                iW, wW, kW = wget(f'WOUT{hb}')
                for cl in range(4):
                    c = hb * 4 + cl
                    b = ps_get('A')
                    proj(b, wW, kW, cl * 128, 128, [AR[:, 28 + k, :] for k in range(8)], ark(*range(28, 36)))
                    S.op('dve', lambda e, b=b, c=c: e.scalar_tensor_tensor(R32[:, c, :], R32[:, c, :], ALPHA, PSB[b][:, :], ALU.mult, ALU.add),
                         reads=psk(b) + [('R32', c)], writes=[('R32', c)])
                wrel(iW)
            S.labels.append(('8', S.seq))
            layernorm(PV_LN1_G, PV_LN1_B)
            if g + 1 < ngroups:
                x_chain(0)
                x_chain(1)
            S.labels.append(('9', S.seq))
            for hf in range(2):
                for q in range(4):
                    iW, wW, kW = wget(f'FF1_{hf * 4 + q}')
                    fb = [ps_get('F') for _ in range(4)]
                    if hf == 0 and q == 0:
                        proj_kc_major(fb, wW, kW, [0, 128, 256, 384])
                    for jl in range(4):
                        j = q * 4 + jl
                        b = fb[jl]
                        if not (hf == 0 and q == 0):
                            proj(b, wW, kW, jl * 128, 128, rb_aps(), rb_keys())
                        tr = j % 4
                        S.op('act', lambda e, b=b, tr=tr: e.activation(TMP[:, tr, :], PSB[b][:, :], AF.Relu), reads=psk(b), writes=tk(tr))
                        S.op('dve', lambda e, tr=tr, j=j: e.tensor_tensor(AR[:, j, :], TMP[:, tr, :], TMP[:, tr, :], ALU.mult), reads=tk(tr), writes=ark(j))
                    wrel(iW)
                for q in range(4):
                    iW, wW, kW = wget(f'FF2_{hf * 4 + q}')
                    for cl in range(2):
                        c = q * 2 + cl
                        b = ps_get('C')
                        proj(b, wW, kW, cl * 128, 128, [AR[:, j, :] for j in range(16)], ark(*range(16)))
                        if hf == 0:
                            S.op('dve', lambda e, b=b, c=c: e.scalar_tensor_tensor(R32[:, c, :], R32[:, c, :], ALPHA, PSB[b][:, :], ALU.mult, ALU.add),
                                 reads=psk(b) + [('R32', c)], writes=[('R32', c)])
                        else:
                            S.op('dve', lambda e, b=b, c=c: e.tensor_tensor(R32[:, c, :], R32[:, c, :], PSB[b][:, :], ALU.add),
                                 reads=psk(b) + [('R32', c)], writes=[('R32', c)])
                    wrel(iW)
            S.labels.append(('10', S.seq))
            for tt in range(4):
                for half in range(2):
                    bb = 2 * tt + half

                    def fn(e, tt=tt, half=half, bb=bb):
                        ins = None
                        for q in range(4):
                            c = half * 4 + q
                            ins = e.transpose(PSB[bb][:, q * P:(q + 1) * P], R32[:, c, tt * P:(tt + 1) * P], IDENT)
                        return ins
                    S.op('pe', fn, reads=[('R32', half * 4 + q) for q in range(4)] + ['cst'], writes=psk(bb))
            for tt in range(4):
                if tt == 2:
                    yield 'tailA'
                xs = (g * 4 + tt) % 2
                stage = TMP[:, 4 + 2 * xs:6 + 2 * xs, :].rearrange("p a b -> p (a b)")
                skeys = tk(4 + 2 * xs, 5 + 2 * xs)
                lk = [('lst2', xs)]
                bks = [2 * tt, 2 * tt + 1]
                for half in range(2):
                    S.op('dve', lambda e, xs=xs, half=half, bb=bks[half]: e.bn_stats(LST2[:, xs, 6 * half:6 * half + 6], PSB[bb][:, :]), reads=psk(bks[half]) + lk, writes=lk)
                S.op('dve', lambda e, xs=xs: e.bn_aggr(LST2[:, xs, 12:14], LST2[:, xs, 0:12]), reads=lk, writes=lk)
                S.op('act', lambda e, xs=xs: e.activation(LST2[:, xs, 14:15], LST2[:, xs, 13:14], AF.Ln, bias=LN_EPS, scale=1.0), reads=lk, writes=lk)
                S.op('act', lambda e, xs=xs: e.activation(LST2[:, xs, 14:15], LST2[:, xs, 14:15], AF.Exp, scale=-0.5), reads=lk, writes=lk)
                S.op('dve', lambda e, xs=xs: e.scalar_tensor_tensor(LST2[:, xs, 15:16], LST2[:, xs, 12:13], -1.0, LST2[:, xs, 14:15], ALU.mult, ALU.mult), reads=lk, writes=lk)
                for half in range(2):
                    S.op('act', lambda e, xs=xs, half=half, bb=bks[half], stage=stage: e.activation(
                        stage[:, half * 512:(half + 1) * 512], PSB[bb][:, :], AF.Identity, bias=LST2[:, xs, 15:16], scale=LST2[:, xs, 14:15]),
                        reads=psk(bks[half]) + lk, writes=[skeys[half]])
                S.op('dve', lambda e, stage=stage: e.tensor_tensor(stage, stage, G2B2[:, 0, :], ALU.mult), reads=skeys + ['g2'], writes=skeys)
                S.op('dve', lambda e, stage=stage: e.tensor_tensor(stage, stage, G2B2[:, 1, :], ALU.add), reads=skeys + ['b2'], writes=skeys)
                S.dma('pool', out_d[t0 + tt * P:t0 + (tt + 1) * P, :], stage, reads=skeys, sem=f'out{xs}', final=True)

        gens = [do_group(g) for g in range(ngroups)]
        next(gens[0])
        for g in range(ngroups):
            next(gens[g])
            if g + 1 < ngroups:
                next(gens[g + 1])
            for _ in gens[g]:
                pass
        if dbg:
            dbg[0](S, locals(), dbg_d)
        S.emit()
    global LAST_SCHED
    LAST_SCHED = S
    return nc


def _blk(Wm):
    K, n = Wm.shape
    kc = K // P
    return Wm.reshape(kc, P, n).transpose(1, 0, 2).reshape(P, kc * n)


def _col(v):
    v = np.asarray(v, np.float32).reshape(-1)
    return v.reshape(-1, P).T


def prepare_shared(inp):
    w_in = np.asarray(inp['w_in'][0], np.float32)
    w_q_b = np.asarray(inp['w_q_b'][0], np.float32)
    w_kv_b = np.asarray(inp['w_kv_b'][0], np.float32)
    w_o_gla = np.asarray(inp['w_o_gla'][0], np.float32)
    w_o_mla = np.asarray(inp['w_o_mla'][0], np.float32)
    w_out = np.asarray(inp['w_out'][0], np.float32)
    w_ff1 = np.asarray(inp['w_ff1'][0], np.float32)
    w_ff2 = np.asarray(inp['w_ff2'][0], np.float32)
    perm = np.concatenate([np.arange(32, 64), np.arange(0, 32)])
    kr = w_in[:, 3728:3792]
    krs = kr[:, perm]
    GATE0 = 3792
    mats = {}
    mats['BA'] = np.concatenate([kr, kr, krs, krs, w_in[:, 3472:3728]], axis=1)
    mats['BQ'] = np.concatenate([w_in[:, 3088:3472], w_in[:, 3072:3088]], axis=1)
    for hb in range(2):
        hs = range(hb * 4, hb * 4 + 4)
        nope = [w_q_b[:, h * 192:h * 192 + 128] for h in hs]
        rope = [w_q_b[:, h * 192 + 128:(h + 1) * 192] for h in hs]
        sw = [r[:, perm] for r in rope]
        mats[f'WQB{hb}'] = np.concatenate(nope + rope + sw, axis=1)
    mats['KVK'] = np.concatenate([w_kv_b[:, h * 256:h * 256 + 128] for h in range(8)], axis=1)
    mats['KVV'] = np.concatenate([w_kv_b[:, h * 256 + 128:(h + 1) * 256] for h in range(8)], axis=1)
    for hb in range(2):
        cs = slice(hb * 512, (hb + 1) * 512)
        mats[f'WOM{hb}'] = w_o_mla[:, cs]
        mats[f'BGM{hb}'] = w_in[:, GATE0 + 1024 + hb * 512:GATE0 + 1024 + (hb + 1) * 512]
        mats[f'WOG{hb}'] = w_o_gla[:, cs]
        mats[f'BGG{hb}'] = w_in[:, GATE0 + hb * 512:GATE0 + (hb + 1) * 512]
        mats[f'WOUT{hb}'] = w_out[:, cs]
        mats[f'BGV{hb}'] = w_in[:, 1024 + hb * 512:1024 + (hb + 1) * 512]
        mats[f'BR{hb}'] = w_in[:, 2048 + hb * 512:2048 + (hb + 1) * 512]
    mats['BGK'] = w_in[:, 512:1024]
    mats['BGQ'] = w_in[:, 0:512]
    for i in range(8):
        mats[f'FF1_{i}'] = w_ff1[:, i * 512:(i + 1) * 512]
        hf, q = i // 4, i % 4
        mats[f'FF2_{i}'] = w_ff2[hf * 2048:(hf + 1) * 2048, q * 256:(q + 1) * 256]
    ws = np.empty((P, WTOT), np.float32)
    for (name, kc, n), off in zip(BLOCKS, BLK_OFF):
        m = mats[name]
        assert m.shape == (kc * P, n), (name, m.shape)
        ws[:, off:off + kc * n] = _blk(m)
    pv = np.zeros((P, NPV), np.float32)
    pv[:, PV_LNIN_G:PV_LNIN_G + 8] = _col(inp['ln_in_g'])
    pv[:, PV_LNIN_B:PV_LNIN_B + 8] = _col(inp['ln_in_b'])
    pv[:, PV_LN1_G:PV_LN1_G + 8] = _col(inp['ln1_g'])
    pv[:, PV_LN1_B:PV_LN1_B + 8] = _col(inp['ln1_b'])
    pv[:, PV_LN2_G:PV_LN2_G + 8] = _col(inp['ln2_g'])
    pv[:, PV_LN2_B:PV_LN2_B + 8] = _col(inp['ln2_b'])
    pv[:, PV_BGATE:PV_BGATE + 16] = _col(inp['b_gate'])
    pv[:, PV_GLAG:PV_GLAG + 8] = _col(inp['gla_norm_g'])
    pv[:, PV_QG:PV_QG + 3] = _col(inp['q_a_norm_g'])
    pv[:, PV_KVG:PV_KVG + 2] = _col(inp['kv_a_norm_g'])
    inv_freq = (1.0 / (10000.0 ** (np.arange(0, 64, 2, dtype=np.float32) / np.float32(64)))).astype(np.float32)
    pidx = np.arange(P)
    pv[:, PV_INVF] = inv_freq[pidx % 32]
    pv[:, PV_SGN] = np.where((pidx % 64) < 32, -1.0, 1.0)
    wa2 = np.concatenate([np.asarray(inp['w_gla_a2'][0], np.float32), np.asarray(inp['b_gla_a2'], np.float32).reshape(1, 512)], axis=0)
    jj, ii = np.meshgrid(np.arange(P), np.arange(P), indexing='ij')
    cst = np.concatenate([np.eye(P, dtype=np.float32), (jj <= ii).astype(np.float32) / 16.0, (jj > ii).astype(np.float32) / 16.0], axis=1)
    lnout = np.ascontiguousarray(np.stack([np.asarray(inp['ln2_g'], np.float32).reshape(-1), np.asarray(inp['ln2_b'], np.float32).reshape(-1)], axis=0))
    return dict(wstream=ws, pvec=pv, wa2=np.ascontiguousarray(wa2), cst=np.ascontiguousarray(cst), lnout=lnout)


_NC_CACHE = {}
LAST_SCHED = None


def kernel(**inputs):
    x = np.asarray(inputs['x'], np.float32)
    pos = np.asarray(inputs['positions'], np.int32)
    shared = prepare_shared(inputs)
    if 'nc' not in _NC_CACHE:
        _NC_CACHE['nc'] = build_program()
    nc = _NC_CACHE['nc']
    in_maps = []
    for b in range(8):
        m = dict(shared)
        m['x'] = np.ascontiguousarray(x[b])
        m['pos'] = np.ascontiguousarray(pos[b].reshape(1, SEQ))
        in_maps.append(m)
    res = run_bass_kernel_spmd(nc, in_maps, core_ids=list(range(8)))
    return np.stack([np.asarray(r['out'], np.float32) for r in res.results], axis=0)
```

```python
import numpy as np
from contextlib import ExitStack
import concourse.bass as bass
import concourse.mybir as mybir
from concourse.bass_utils import run_bass_kernel_spmd

F32 = mybir.dt.float32
BF16 = mybir.dt.bfloat16
I32 = mybir.dt.int32
AF = mybir.ActivationFunctionType
ALU = mybir.AluOpType


class Sched:
    ENGS = ('pe', 'act', 'dve', 'pool', 'sp')

    def __init__(self, nc):
        self.nc = nc
        self.ops = {e: [] for e in self.ENGS}
        self.count = {e: 0 for e in self.ENGS}
        self.waited = {e: {} for e in self.ENGS}
        self.last_write = {}
        self.readers = {}
        self.dma_count = {}
        self.final_tokens = []
        self.seq = 0
        self.cut = None
        self.labels = []
        self.bank_last = {}

    def _deps(self, eng, reads, writes):
        deps = {}

        def add(tok):
            if tok is None:
                return
            s, v = tok
            if deps.get(s, 0) < v:
                deps[s] = v
        for k in reads:
            add(self.last_write.get(k))
        for k in writes:
            add(self.last_write.get(k))
            for r in self.readers.get(k, ()):
                add(r)
        for k in list(reads) + list(writes):
            if isinstance(k, tuple) and k[0] == 'ps':
                for e2, tok in self.bank_last.get(k[1], {}).items():
                    if e2 != eng:
                        add(tok)
        waits = []
        for s, v in deps.items():
            if eng == 'pe' and s == 'S_pe':
                continue
            if self.waited[eng].get(s, 0) >= v:
                continue
            self.waited[eng][s] = v
            waits.append((s, v))
        return waits

    def _commit(self, tok, reads, writes, eng=None):
        for k in list(reads) + list(writes):
            if isinstance(k, tuple) and k[0] == 'ps':
                self.bank_last.setdefault(k[1], {})[eng] = tok
        for k in reads:
            self.readers.setdefault(k, []).append(tok)
        for k in writes:
            self.last_write[k] = tok
            self.readers[k] = []

    def op(self, eng, fn, reads=(), writes=()):
        waits = self._deps(eng, reads, writes)
        self.count[eng] += 1
        tok = ('S_' + eng, self.count[eng])
        self.ops[eng].append((waits, fn, tok[0], 1, self.seq))
        self.seq += 1
        self._commit(tok, reads, writes, eng)
        return tok

    def dma(self, eng, out, in_, reads=(), writes=(), sem=None, final=False):
        waits = self._deps(eng, reads, writes)
        sname = 'D_' + sem
        self.dma_count[sname] = self.dma_count.get(sname, 0) + 16
        tok = (sname, self.dma_count[sname])
        self.ops[eng].append((waits, lambda e: e.dma_start(out=out, in_=in_), sname, 16, self.seq))
        self.seq += 1
        self._commit(tok, reads, writes)
        if final:
            self.final_tokens.append(tok + (self.seq - 1,))
        return tok

    def emit(self):
        nc = self.nc
        names = ['S_' + e for e in self.ENGS if e != 'sp'] + sorted(self.dma_count)
        with ExitStack() as st:
            sems = {n: st.enter_context(nc.semaphore(n)) for n in names}
            block = st.enter_context(nc.Block())

            def run(e, h):
                for waits, fn, sname, inc, seq in self.ops[e]:
                    if self.cut is not None and seq >= self.cut:
                        break
                    for s, v in waits:
                        h.wait_ge(sems[s], v)
                    fn(h).then_inc(sems[sname], inc)
                if e == 'sp':
                    fin = {}
                    for e2 in self.ENGS:
                        for waits, fn, sname, inc, seq in self.ops[e2]:
                            if inc == 16 and (self.cut is None or seq < self.cut):
                                fin[sname] = fin.get(sname, 0) + 16
                    for s, v, seq in self.final_tokens:
                        if self.cut is not None and seq >= self.cut:
                            continue
                        fin[s] = max(fin.get(s, 0), v)
                    for s, v in fin.items():
                        h.wait_ge(sems[s], v)

            @block.tensor
            def _(h):
                run('pe', h)

            @block.scalar
            def _(h):
                run('act', h)

            @block.vector
            def _(h):
                run('dve', h)

            @block.gpsimd
            def _(h):
                run('pool', h)

            @block.sync
            def _(h):
                run('sp', h)


P = 128
TG = 512
NG = 4
SEQ = 2048
D = 1024
NSLOT = 4
WELEMS = 4096
NPV = 80
ALPHA = 2.0 ** 0.25
LN_EPS = 1e-5
RMS_EPS = 1e-6
ATT_SCALE = 192.0 ** -0.5
GLA_QSCALE = 128.0 ** -0.5
TWO_PI = 2.0 * np.pi
CW1 = 6.28125
CW2 = TWO_PI - CW1

PV_LNIN_G, PV_LNIN_B, PV_LN1_G, PV_LN1_B, PV_LN2_G, PV_LN2_B = 0, 8, 16, 24, 32, 40
PV_BGATE, PV_GLAG, PV_QG, PV_KVG, PV_INVF, PV_SGN = 48, 64, 72, 75, 77, 78

BLOCKS = [
    ('BA', 8, 512), ('BQ', 8, 400), ('KVK', 2, 1024), ('KVV', 2, 1024),
    ('WQB0', 3, 1024), ('WQB1', 3, 1024),
    ('WOM0', 8, 512), ('BGM0', 8, 512), ('WOM1', 8, 512), ('BGM1', 8, 512),
    ('BGK', 8, 512), ('BGQ', 8, 512), ('BGV0', 8, 512), ('BGV1', 8, 512),
    ('BR0', 8, 512), ('BR1', 8, 512),
    ('WOG0', 8, 512), ('BGG0', 8, 512), ('WOG1', 8, 512), ('BGG1', 8, 512),
    ('WOUT0', 8, 512), ('WOUT1', 8, 512),
    ('FF1_0', 8, 512), ('FF1_1', 8, 512), ('FF1_2', 8, 512), ('FF1_3', 8, 512),
    ('FF2_0', 16, 256), ('FF2_1', 16, 256), ('FF2_2', 16, 256), ('FF2_3', 16, 256),
    ('FF1_4', 8, 512), ('FF1_5', 8, 512), ('FF1_6', 8, 512), ('FF1_7', 8, 512),
    ('FF2_4', 16, 256), ('FF2_5', 16, 256), ('FF2_6', 16, 256), ('FF2_7', 16, 256),
]
BLK_OFF = []
_o = 0
for _n, _kc, _nn in BLOCKS:
    BLK_OFF.append(_o)
    _o += _kc * _nn
WTOT = _o


def build_program(dbg=None, ngroups=NG, cut=None, marks=None):
    nc = bass.Bass("TRN2", target_bir_lowering=False)
    x_d = nc.dram_tensor("x", [SEQ, D], F32, kind="ExternalInput").ap()
    pos_d = nc.dram_tensor("pos", [1, SEQ], I32, kind="ExternalInput").ap()
    ws_d = nc.dram_tensor("wstream", [P, WTOT], F32, kind="ExternalInput").ap()
    pv_d = nc.dram_tensor("pvec", [P, NPV], F32, kind="ExternalInput").ap()
    wa2_d = nc.dram_tensor("wa2", [17, 512], F32, kind="ExternalInput").ap()
    cst_d = nc.dram_tensor("cst", [P, 384], F32, kind="ExternalInput").ap()
    lno_d = nc.dram_tensor("lnout", [2, D], F32, kind="ExternalInput").ap()
    out_d = nc.dram_tensor("out", [SEQ, D], F32, kind="ExternalOutput").ap()
    if dbg:
        dbg_d = nc.dram_tensor("dbg", [P, dbg[1]], F32, kind="ExternalOutput").ap()

    with ExitStack() as st:
        def sb(name, shape, dt):
            return st.enter_context(nc.sbuf_tensor(name, shape, dt))

        KT = sb("KT", [P, 8, SEQ], BF16)
        KR = sb("KR", [P, SEQ], BF16)
        V = sb("V", [P, 16, 1024], BF16)
        Sst = sb("Sst", [P, 4, 256], F32)
        Sbf = sb("Sbf", [P, 4, 256], BF16)
        WS = [sb(f"ws{i}", [P, WELEMS], BF16) for i in range(NSLOT)]
        CST = sb("CST", [P, 384], F32)
        MU = sb("MU", [P, 128], BF16)
        ONES = sb("ONES", [P, 128], BF16)
        PV = sb("PV", [P, NPV], F32)
        WA2 = sb("WA2", [17, 512], F32)
        A17 = sb("A17", [17, 512], F32)
        NBG = sb("NBG", [P, 16], F32)
        G2B2 = sb("G2B2", [P, 2, D], F32)
        LST2 = sb("LST2", [P, 2, 16], F32)
        LST = sb("LST", [P, 2, 16], F32)
        R32 = sb("R32", [P, 8, TG], F32)
        Rb = sb("Rb", [P, 8, TG], BF16)
        XIN = [sb(f"xin{i}", [P, D], F32) for i in range(2)]
        NS = 38
        AR = sb("AR", [P, NS, TG], BF16)
        NT = 8
        TMP = sb("TMP", [P, NT, TG], F32)
        SM = sb("SM", [P, 4, 128], BF16)
        PSB = [st.enter_context(nc.psum_tensor(f"psb{i}", [P, 512], F32)) for i in range(8)]

        POSB = TMP[:, 7, :].bitcast(I32)
        IDENT = CST[:, 0:128]
        UF = CST[:, 128:256]
        LF = CST[:, 256:384]

        S = Sched(nc)
        S.cut = cut

        def psk(b):
            return [('ps', b, q) for q in range(4)]

        def ark(*idx):
            return [('ar', i) for i in idx]

        def tk(*idx):
            return [('tmp', i) for i in idx]

        def pvc(col):
            return PV[:, col:col + 1]

        pools = {'A': [0, 1, 2, 3], 'S': [0, 1, 2, 3], 'B': [4, 5, 6, 7], 'C': [6, 7], 'G': [0, 1, 2, 3, 4, 5, 6, 7], 'F': [0, 1, 2, 3, 4, 5]}
        pool_pos = {k: 0 for k in pools}

        def ps_get(pool):
            b = pools[pool][pool_pos[pool] % len(pools[pool])]
            pool_pos[pool] += 1
            return b

        scr_pos = [0]

        def scr_get():
            i = 36 + (scr_pos[0] % 2)
            scr_pos[0] += 1
            return i

        ln_scr = [16, 17, 18, 19, 36, 37]
        ln_pos = [0]

        def ln_scr_get():
            i = ln_scr[ln_pos[0] % len(ln_scr)]
            ln_pos[0] += 1
            return i

        evac_rr = [0]

        def evac_copy(out_ap, in_ap, reads, writes, eng=None):
            if eng is None:
                eng = 'act' if evac_rr[0] % 2 == 0 else 'dve'
                evac_rr[0] += 1
            if eng == 'act':
                S.op('act', lambda e: e.activation(out_ap, in_ap, AF.Copy), reads, writes)
            else:
                S.op('dve', lambda e: e.tensor_copy(out_ap, in_ap), reads, writes)

        def mm(ps_ap, pairs, reads, writes):
            def fn(e):
                n = len(pairs)
                ins = None
                for i, (l, r) in enumerate(pairs):
                    ins = e.matmul(ps_ap, l, r, start=(i == 0), stop=(i == n - 1))
                return ins
            S.op('pe', fn, reads, writes)

        NB = len(BLOCKS)
        total_blocks = ngroups * NB
        wstate = {'next_dma': 0, 'next_use': 0}
        released = [False] * total_blocks

        def w_issue():
            while wstate['next_dma'] < total_blocks and (
                    wstate['next_dma'] < NSLOT or released[wstate['next_dma'] - NSLOT]):
                i = wstate['next_dma']
                name, kc, n = BLOCKS[i % NB]
                off = BLK_OFF[i % NB]
                s = i % NSLOT
                S.dma('pool', WS[s][:, 0:kc * n], ws_d[:, off:off + kc * n],
                      reads=([('xin', 0), ('xin', 1), 'cst', 'pv'] if i == 0 else []), writes=[('w', s)], sem=f'w{s}')
                wstate['next_dma'] += 1

        def wget(name):
            i = wstate['next_use']
            bname, kc, n = BLOCKS[i % NB]
            assert bname == name, (bname, name)
            wstate['next_use'] += 1
            s = i % NSLOT
            view = WS[s][:, 0:kc * n].rearrange("p (k n) -> p k n", k=kc)
            return i, view, ('w', s)

        def wrel(i):
            released[i] = True
            w_issue()

        for tt0 in range(2):
            S.dma('sp', XIN[tt0][:, :], x_d[tt0 * P:(tt0 + 1) * P, :], writes=[('xin', tt0)], sem=f'xin{tt0}')
        S.dma('sp', CST[:, :], cst_d[:, :], writes=['cst'], sem='cst')
        S.dma('sp', PV[:, :], pv_d[:, :], writes=['pv'], sem='pv')
        S.dma('sp', WA2[:, :], wa2_d[:, :], writes=['wa2'], sem='wa2')
        S.dma('sp', G2B2[:, 0, :], lno_d[0:1, :].partition_broadcast(P), writes=['g2'], sem='g2')
        S.dma('sp', G2B2[:, 1, :], lno_d[1:2, :].partition_broadcast(P), writes=['b2'], sem='b2')
        S.op('dve', lambda e: e.memset(ONES[:, :], 1.0), writes=['ones'])
        S.op('dve', lambda e: e.memset(A17[:, :], 1.0), writes=['a17'])
        S.op('dve', lambda e: e.memset(Sst[:, :, :], 0.0), writes=[('Sst', h) for h in range(4)])
        S.op('dve', lambda e: e.memset(Sbf[:, :, :], 0.0), writes=[('Sbf', h) for h in range(4)])
        S.op('dve', lambda e: e.tensor_copy(MU[:, :], UF), reads=['cst'], writes=['mu'])
        S.op('dve', lambda e: e.tensor_scalar(MU[:, :], MU[:, :], 16.0, None, ALU.mult), reads=['mu'], writes=['mu'])
        S.op('dve', lambda e: e.tensor_scalar(NBG[:, :], PV[:, PV_BGATE:PV_BGATE + 16], -1.0, None, ALU.mult), reads=['pv'], writes=['nbg'])
        w_issue()

        def layernorm(gcol, bcol, want_bf16=True):
            b_sum, b_sq = 6, 7
            for c in range(8):
                s1, s2 = ln_scr_get(), ln_scr_get()
                S.op('act', lambda e, c=c, s1=s1: e.activation(AR[:, s1, :], R32[:, c, :], AF.Copy),
                     reads=[('R32', c)], writes=ark(s1))
                S.op('act', lambda e, c=c, s2=s2: e.activation(AR[:, s2, :], R32[:, c, :], AF.Square),
                     reads=[('R32', c)], writes=ark(s2))
                S.op('pe', lambda e, c=c, s1=s1: e.matmul(PSB[b_sum][:, :], ONES[:, :], AR[:, s1, :], start=(c == 0), stop=(c == 7)),
                     reads=ark(s1) + ['ones'] + psk(b_sum), writes=psk(b_sum))
                S.op('pe', lambda e, c=c, s2=s2: e.matmul(PSB[b_sq][:, :], ONES[:, :], AR[:, s2, :], start=(c == 0), stop=(c == 7)),
                     reads=ark(s2) + ['ones'] + psk(b_sq), writes=psk(b_sq))
            T_mean, T_r, T_m2 = TMP[:, 0, :], TMP[:, 1, :], TMP[:, 2, :]
            S.op('dve', lambda e: e.tensor_scalar(T_mean, PSB[b_sum][:, :], 1.0 / D, None, ALU.mult),
                 reads=psk(b_sum), writes=tk(0))
            S.op('dve', lambda e: e.tensor_tensor(T_m2, T_mean, T_mean, ALU.mult), reads=tk(0), writes=tk(2))
            S.op('dve', lambda e: e.scalar_tensor_tensor(T_r, PSB[b_sq][:, :], 1.0 / D, T_m2, ALU.mult, ALU.subtract),
                 reads=psk(b_sq) + tk(2), writes=tk(1))
            S.op('act', lambda e: e.activation(T_r, T_r, AF.Ln, bias=LN_EPS, scale=1.0), reads=tk(1), writes=tk(1))
            S.op('act', lambda e: e.activation(T_r, T_r, AF.Exp, scale=-0.5), reads=tk(1), writes=tk(1))
            S.op('dve', lambda e: e.scalar_tensor_tensor(T_m2, T_mean, -1.0, T_r, ALU.mult, ALU.mult), reads=tk(0, 1), writes=tk(2))
            for c in range(8):
                S.op('dve', lambda e, c=c: e.tensor_tensor(R32[:, c, :], R32[:, c, :], T_r, ALU.mult),
                     reads=[('R32', c)] + tk(1), writes=[('R32', c)])
                S.op('dve', lambda e, c=c: e.tensor_tensor(R32[:, c, :], R32[:, c, :], T_m2, ALU.add),
                     reads=[('R32', c)] + tk(2), writes=[('R32', c)])
                if want_bf16:
                    S.op('act', lambda e, c=c: e.activation(Rb[:, c, :], R32[:, c, :], AF.Identity, bias=pvc(bcol + c), scale=pvc(gcol + c)),
                         reads=[('R32', c), 'pv'], writes=[('Rb', c)])
            for c in range(8):
                S.op('act', lambda e, c=c: e.activation(R32[:, c, :], R32[:, c, :], AF.Identity, bias=pvc(bcol + c), scale=pvc(gcol + c)),
                     reads=[('R32', c), 'pv'], writes=[('R32', c)])

        def proj(bank, wv, wkey, col0, m, rhs_aps, rhs_keys, prow=P):
            pairs = [(wv[:, kc, col0:col0 + m], rhs_aps[kc]) for kc in range(len(rhs_aps))]
            mm(PSB[bank][0:m, :], pairs, reads=[wkey] + rhs_keys, writes=psk(bank))

        def proj_kc_major(banks, wv, wkey, col0s, m=128):
            for kc in range(8):
                for bank, col0 in zip(banks, col0s):
                    S.op('pe', lambda e, bank=bank, col0=col0, kc=kc: e.matmul(PSB[bank][0:m, :], wv[:, kc, col0:col0 + m], Rb[:, kc, :],
                                                                              start=(kc == 0), stop=(kc == 7)),
                         reads=[wkey, ('Rb', kc)] + psk(bank), writes=psk(bank))

        def x_chain(xs):
            lk = [('lst', xs)]
            S.op('dve', lambda e: e.bn_stats(LST[:, xs, 0:6], XIN[xs][:, 0:512]), reads=[('xin', xs)], writes=lk)
            S.op('dve', lambda e: e.bn_stats(LST[:, xs, 6:12], XIN[xs][:, 512:1024]), reads=[('xin', xs)] + lk, writes=lk)
            S.op('dve', lambda e: e.bn_aggr(LST[:, xs, 12:14], LST[:, xs, 0:12]), reads=lk, writes=lk)
            S.op('act', lambda e: e.activation(LST[:, xs, 14:15], LST[:, xs, 13:14], AF.Ln, bias=LN_EPS, scale=1.0), reads=lk, writes=lk)
            S.op('act', lambda e: e.activation(LST[:, xs, 14:15], LST[:, xs, 14:15], AF.Exp, scale=-0.5), reads=lk, writes=lk)
            S.op('dve', lambda e: e.scalar_tensor_tensor(LST[:, xs, 15:16], LST[:, xs, 12:13], -1.0, LST[:, xs, 14:15], ALU.mult, ALU.mult), reads=lk, writes=lk)
            S.op('act', lambda e: e.activation(XIN[xs][:, :], XIN[xs][:, :], AF.Identity, bias=LST[:, xs, 15:16], scale=LST[:, xs, 14:15]),
                 reads=[('xin', xs)] + lk, writes=[('xin', xs)])

        def rb_aps():
            return [Rb[:, kc, :] for kc in range(8)]

        def rb_keys():
            return [('Rb', kc) for kc in range(8)]

        def rms_rstd(ps_banks, nfeat, t_out):
            b_stat = 7
            n = len(ps_banks)
            for i, b in enumerate(ps_banks):
                s2 = scr_get()
                S.op('act', lambda e, b=b, s2=s2: e.activation(AR[:, s2, :], PSB[b][:, :], AF.Square),
                     reads=psk(b), writes=ark(s2))
                S.op('pe', lambda e, i=i, s2=s2: e.matmul(PSB[b_stat][:, :], ONES[:, :], AR[:, s2, :], start=(i == 0), stop=(i == n - 1)),
                     reads=ark(s2) + ['ones'] + psk(b_stat), writes=psk(b_stat))
            T = TMP[:, t_out, :]
            S.op('act', lambda e: e.activation(T, PSB[b_stat][:, :], AF.Ln, bias=RMS_EPS, scale=1.0 / nfeat),
                 reads=psk(b_stat), writes=tk(t_out))
            S.op('act', lambda e: e.activation(T, T, AF.Exp, scale=-0.5), reads=tk(t_out), writes=tk(t_out))

        def do_group(g):
            t0 = g * TG
            S.labels.append(('1', S.seq))
            for tt in range(4):
                xs = (g * 4 + tt) % 2
                if tt >= 2:
                    S.dma('sp', XIN[xs][:, :], x_d[t0 + tt * P:t0 + (tt + 1) * P, :], writes=[('xin', xs)], sem=f'xin{xs}')
                if g == 0 or tt >= 2:
                    x_chain(xs)
                for half in range(2):
                    b = ps_get('A')

                    def fn(e, xs=xs, half=half, b=b):
                        ins = None
                        for q in range(4):
                            c = half * 4 + q
                            ins = e.transpose(PSB[b][:, q * P:(q + 1) * P], XIN[xs][:, c * P:(c + 1) * P], IDENT)
                        return ins
                    S.op('pe', fn, reads=[('xin', xs), 'cst'], writes=psk(b))
                    evac_copy(R32[:, half * 4:half * 4 + 4, tt * P:(tt + 1) * P],
                              PSB[b][:, :].rearrange("p (a b) -> p a b", a=4),
                              reads=psk(b), writes=[('R32', half * 4 + q) for q in range(4)])
                if tt == 1:
                    yield 'headA'
            S.labels.append(('2', S.seq))
            for c in range(8):
                if c % 2 == 0:
                    S.op('act', lambda e, c=c: e.activation(Rb[:, c, :], R32[:, c, :], AF.Identity, bias=pvc(PV_LNIN_B + c), scale=pvc(PV_LNIN_G + c)),
                         reads=[('R32', c), 'pv'], writes=[('Rb', c)])
                else:
                    S.op('dve', lambda e, c=c: e.tensor_scalar(Rb[:, c, :], R32[:, c, :], pvc(PV_LNIN_G + c), pvc(PV_LNIN_B + c), ALU.mult, ALU.add),
                         reads=[('R32', c), 'pv'], writes=[('Rb', c)])

            def deferred_r32_affine():
                for c in range(8):
                    S.op('dve', lambda e, c=c: e.tensor_scalar(R32[:, c, :], R32[:, c, :], pvc(PV_LNIN_G + c), pvc(PV_LNIN_B + c), ALU.mult, ALU.add),
                         reads=[('R32', c), 'pv'], writes=[('R32', c)])

            S.labels.append(('0', S.seq))
            S.dma('sp', POSB, pos_d[0:1, t0:t0 + TG].partition_broadcast(P), writes=tk(7), sem='pos')
            T_ang, T_k, T_c, T_s = TMP[:, 5, :], TMP[:, 6, :], TMP[:, 3, :], TMP[:, 4, :]
            KI = POSB
            S.op('dve', lambda e: e.tensor_copy(T_ang, POSB), reads=tk(7), writes=tk(5))
            S.op('dve', lambda e: e.tensor_scalar(T_ang, T_ang, pvc(PV_INVF), None, ALU.mult), reads=tk(5) + ['pv'], writes=tk(5))
            S.op('dve', lambda e: e.tensor_scalar(T_k, T_ang, 1.0 / TWO_PI, None, ALU.mult), reads=tk(5), writes=tk(6))
            S.op('dve', lambda e: e.tensor_copy(KI, T_k), reads=tk(6), writes=tk(7))
            S.op('dve', lambda e: e.tensor_copy(T_k, KI), reads=tk(7), writes=tk(6))
            S.op('dve', lambda e: e.scalar_tensor_tensor(T_ang, T_k, -CW1, T_ang, ALU.mult, ALU.add), reads=tk(5, 6), writes=tk(5))
            S.op('dve', lambda e: e.scalar_tensor_tensor(T_ang, T_k, -CW2, T_ang, ALU.mult, ALU.add), reads=tk(5, 6), writes=tk(5))
            S.op('dve', lambda e: e.tensor_scalar(T_ang, T_ang, float(np.pi), float(-np.pi), ALU.min, ALU.max), reads=tk(5), writes=tk(5))
            S.op('act', lambda e: e.activation(T_s, T_ang, AF.Sin), reads=tk(5), writes=tk(4))
            S.op('dve', lambda e: e.tensor_scalar(T_s, T_s, pvc(PV_SGN), None, ALU.mult), reads=tk(4) + ['pv'], writes=tk(4))
            S.op('dve', lambda e: e.scalar_tensor_tensor(T_k, T_ang, -1.0, T_ang, ALU.mult, ALU.max), reads=tk(5), writes=tk(6))
            S.op('act', lambda e: e.activation(T_c, T_k, AF.Sin, bias=float(np.pi / 2), scale=-1.0), reads=tk(6), writes=tk(3))

            S.labels.append(('3', S.seq))
            iBA, wBA, kBA = wget('BA')
            b_kr, b_krs = ps_get('A'), ps_get('A')
            b_c = [ps_get('A'), ps_get('A')]
            proj_kc_major([b_kr, b_krs, b_c[0], b_c[1]], wBA, kBA, [0, 128, 256, 384])
            S.op('dve', lambda e: e.tensor_tensor(TMP[:, 5, :], PSB[b_kr][:, :], T_c, ALU.mult), reads=psk(b_kr) + tk(3), writes=tk(5))
            S.op('dve', lambda e: e.tensor_tensor(TMP[:, 6, :], PSB[b_krs][:, :], T_s, ALU.mult), reads=psk(b_krs) + tk(4), writes=tk(6))
            S.op('dve', lambda e: e.tensor_tensor(KR[:, t0:t0 + TG], TMP[:, 5, :], TMP[:, 6, :], ALU.add), reads=tk(5, 6), writes=[('KR', g)])
            wrel(iBA)
            CK32 = [XIN[0][:, 0:512], XIN[0][:, 512:1024]]
            QL32 = [XIN[1][:, 0:512], XIN[1][:, 512:1024], TMP[:, 7, :]]
            QLK = [[('xin', 1)], [('xin', 1)], tk(7)]
            for i in range(2):
                evac_copy(CK32[i], PSB[b_c[i]][:, :], reads=psk(b_c[i]), writes=[('xin', 0)], eng='act')
            rms_rstd(b_c, 256, 1)
            for i in range(2):
                S.op('dve', lambda e, i=i: e.scalar_tensor_tensor(AR[:, i, :], CK32[i], pvc(PV_KVG + i), TMP[:, 1, :], ALU.mult, ALU.mult),
                     reads=tk(1) + [('xin', 0), 'pv'], writes=ark(i))
            iBQ, wBQ, kBQ = wget('BQ')
            b_q = [ps_get('A'), ps_get('A'), ps_get('A')]
            for i in range(3):
                proj(b_q[i], wBQ, kBQ, i * 128, 128, rb_aps(), rb_keys())
            for i in range(3):
                evac_copy(QL32[i], PSB[b_q[i]][:, :], reads=psk(b_q[i]), writes=QLK[i], eng='act')
            rms_rstd(b_q, 384, 1)
            for i in range(3):
                S.op('dve', lambda e, i=i: e.scalar_tensor_tensor(AR[:, 2 + i, :], QL32[i], pvc(PV_QG + i), TMP[:, 1, :], ALU.mult, ALU.mult),
                     reads=tk(1) + QLK[i] + ['pv'], writes=ark(2 + i))
            b_a = ps_get('A')
            proj(b_a, wBQ, kBQ, 384, 16, rb_aps(), rb_keys())
            wrel(iBQ)
            S.op('dve', lambda e: e.tensor_copy(A17[0:16, :], PSB[b_a][0:16, :]), reads=psk(b_a) + ['a17'], writes=['a17'])
            ck_aps = [AR[:, 0, :], AR[:, 1, :]]
            ck_keys = ark(0, 1)
            iW, wW, kW = wget('KVK')
            for h in range(8):
                b = ps_get('F')
                proj(b, wW, kW, h * 128, 128, ck_aps, ck_keys)
                evac_copy(KT[:, h, t0:t0 + TG], PSB[b][:, :], reads=psk(b), writes=[('KT', h, g)], eng='act')
            wrel(iW)
            iW, wW, kW = wget('KVV')
            for tt in range(4):
                for half in range(2):
                    b = ps_get('F')
                    pairs = [(AR[:, kc, tt * P:(tt + 1) * P], wW[:, kc, half * 512:(half + 1) * 512]) for kc in range(2)]
                    mm(PSB[b][:, :], pairs, reads=[kW] + ck_keys, writes=psk(b))
                    evac_copy(V[:, g * 4 + tt, half * 512:(half + 1) * 512], PSB[b][:, :], reads=psk(b), writes=[('V', g * 4 + tt, half)], eng=('act' if half == 0 else None))
            wrel(iW)

            QR = [13, 14, 15, 16, 0, 1, 36, 37]
            qn_aps = [AR[:, 2 + i, :] for i in range(3)]
            qn_keys = ark(2, 3, 4)
            for hb in range(2):
                iW, wW, kW = wget(f'WQB{hb}')
                for hl in range(4):
                    h = hb * 4 + hl
                    b = ps_get('F')
                    proj(b, wW, kW, hl * 128, 128, qn_aps, qn_keys)
                    evac_copy(AR[:, 5 + h, :], PSB[b][:, :], reads=psk(b), writes=ark(5 + h), eng='act')
                for pl in range(2):
                    pr = hb * 2 + pl
                    b_r, b_s = ps_get('F'), ps_get('F')
                    proj(b_r, wW, kW, 512 + pl * 128, 128, qn_aps, qn_keys)
                    proj(b_s, wW, kW, 768 + pl * 128, 128, qn_aps, qn_keys)
                    S.op('dve', lambda e, b_r=b_r: e.tensor_tensor(TMP[:, 5, :], PSB[b_r][:, :], T_c, ALU.mult), reads=psk(b_r) + tk(3), writes=tk(5))
                    S.op('dve', lambda e, b_s=b_s: e.tensor_tensor(TMP[:, 6, :], PSB[b_s][:, :], T_s, ALU.mult), reads=psk(b_s) + tk(4), writes=tk(6))
                    sa, sb_ = QR[2 * pr], QR[2 * pr + 1]
                    S.op('dve', lambda e, sa=sa: e.tensor_tensor(AR[0:64, sa, :], TMP[0:64, 5, :], TMP[0:64, 6, :], ALU.add), reads=tk(5, 6), writes=ark(sa))
                    S.op('act', lambda e, sa=sa: e.memzero(AR[64:128, sa, :]), writes=ark(sa))
                    S.op('dve', lambda e, sb_=sb_: e.tensor_tensor(AR[64:128, sb_, :], TMP[64:128, 5, :], TMP[64:128, 6, :], ALU.add), reads=tk(5, 6), writes=ark(sb_))
                    S.op('act', lambda e, sb_=sb_: e.memzero(AR[0:64, sb_, :]), writes=ark(sb_))
                wrel(iW)
            S.labels.append(('4', S.seq))
            blocks = [(h, kt) for h in range(8) for kt in range(4 * g + 4)]
            nkt = 4 * g + 4

            def score_block(h, kt):
                r = kt - 4 * g
                q0 = P * max(r, 0)
                b = ps_get('S')
                qs = QR[h]

                def fn(e):
                    e.matmul(PSB[b][:, q0:TG], KT[:, h, kt * P:(kt + 1) * P], AR[:, 5 + h, q0:TG], start=True, stop=False)
                    return e.matmul(PSB[b][:, q0:TG], KR[:, kt * P:(kt + 1) * P], AR[:, qs, q0:TG], start=False, stop=True)
                S.op('pe', fn, reads=[('KT', h, kt // 4), ('KR', kt // 4)] + ark(5 + h, qs), writes=psk(b))
                return b, q0, r

            deferred = []
            pendq = [score_block(*blocks[0])]
            for bj in (1, 2):
                if len(blocks) > bj:
                    pendq.append(score_block(*blocks[bj]))
            pt_pos = 0
            for bi, (h, kt) in enumerate(blocks):
                b, q0, r = pendq.pop(0)
                if bi + 3 < len(blocks):
                    pendq.append(score_block(*blocks[bi + 3]))
                pt = 17 + (pt_pos % 3)
                pt_pos += 1
                S.op('act', lambda e, b=b, q0=q0, pt=pt: e.activation(AR[:, pt, q0:TG], PSB[b][:, q0:TG], AF.Exp, scale=ATT_SCALE),
                     reads=psk(b), writes=ark(pt))
                if r >= 0:
                    S.op('dve', lambda e, q0=q0, pt=pt: e.tensor_tensor(AR[:, pt, q0:q0 + P], AR[:, pt, q0:q0 + P], MU[:, :], ALU.mult),
                         reads=ark(pt) + ['mu'], writes=ark(pt))
                b_o = 4 + (h % 2)
                b_s = 6 + (h % 2)
                S.op('pe', lambda e, h=h, kt=kt, q0=q0, pt=pt, b_o=b_o: e.matmul(
                    PSB[b_o][:, q0:TG], V[:, kt, h * P:(h + 1) * P], AR[:, pt, q0:TG], start=(kt == 0), stop=(kt == nkt - 1)),
                    reads=ark(pt) + [('V', kt, h // 4)] + psk(b_o), writes=psk(b_o))
                S.op('pe', lambda e, kt=kt, q0=q0, pt=pt, b_s=b_s: e.matmul(
                    PSB[b_s][:, q0:TG], ONES[:, :], AR[:, pt, q0:TG], start=(kt == 0), stop=(kt == nkt - 1)),
                    reads=ark(pt) + ['ones'] + psk(b_s), writes=psk(b_s))
                for fdef in deferred:
                    fdef()
                deferred.clear()
                if bi == 1:
                    deferred_r32_affine()
                if kt == nkt - 1:
                    def head_end(h=h, b_o=b_o, b_s=b_s):
                        tr = h % 2
                        S.op('act', lambda e: e.activation(TMP[:, tr, :], PSB[b_s][:, :], AF.Ln), reads=psk(b_s), writes=tk(tr))
                        S.op('act', lambda e: e.activation(TMP[:, tr, :], TMP[:, tr, :], AF.Exp, scale=-1.0), reads=tk(tr), writes=tk(tr))
                        S.op('dve', lambda e: e.tensor_tensor(AR[:, 20 + h, :], PSB[b_o][:, :], TMP[:, tr, :], ALU.mult),
                             reads=psk(b_o) + tk(tr), writes=ark(20 + h))
                    deferred.append(head_end)
            for fdef in deferred:
                fdef()
            deferred.clear()

            S.labels.append(('5', S.seq))
            def out_proj_gated(wname, gname, x_slots, gate_col0, add_prev):
                for hb in range(2):
                    iW, wW, kW = wget(f'{wname}{hb}')
                    iG, wG, kG = wget(f'{gname}{hb}')
                    for cl in range(4):
                        c = hb * 4 + cl
                        b_g, b_y = ps_get('G'), ps_get('G')
                        proj(b_g, wG, kG, cl * 128, 128, rb_aps(), rb_keys())
                        proj(b_y, wW, kW, cl * 128, 128, [AR[:, s, :] for s in x_slots], ark(*x_slots))
                        tg_ = 2 + (c % 2)
                        S.op('act', lambda e, b_g=b_g, tg_=tg_, c=c: e.activation(TMP[:, tg_, :], PSB[b_g][:, :], AF.Sigmoid, bias=pvc(PV_BGATE + gate_col0 + c), scale=1.0),
                             reads=psk(b_g) + ['pv'], writes=tk(tg_))
                        if not add_prev:
                            S.op('dve', lambda e, b_y=b_y, tg_=tg_, c=c: e.tensor_tensor(AR[:, 28 + c, :], PSB[b_y][:, :], TMP[:, tg_, :], ALU.mult),
                                 reads=psk(b_y) + tk(tg_), writes=ark(28 + c))
                        else:
                            S.op('dve', lambda e, b_y=b_y, tg_=tg_: e.tensor_tensor(TMP[:, tg_, :], PSB[b_y][:, :], TMP[:, tg_, :], ALU.mult),
                                 reads=psk(b_y) + tk(tg_), writes=tk(tg_))
                            S.op('dve', lambda e, tg_=tg_, c=c: e.tensor_tensor(AR[:, 28 + c, :], TMP[:, tg_, :], AR[:, 28 + c, :], ALU.add),
                                 reads=tk(tg_) + ark(28 + c), writes=ark(28 + c))
                    wrel(iW)
                    wrel(iG)

            out_proj_gated('WOM', 'BGM', list(range(20, 28)), 8, add_prev=False)

            S.labels.append(('6', S.seq))
            iK, wK, kK = wget('BGK')
            EB = lambda h: TMP[:, 4 + h, :]
            ENB = lambda h: XIN[h // 2][:, (h % 2) * 512:(h % 2 + 1) * 512]
            b_zs = [ps_get('G') for _ in range(4)]
            for tt in range(4):
                S.op('pe', lambda e, b_z=b_zs[tt], tt=tt: e.matmul(PSB[b_z][:, :], A17[0:17, tt * P:(tt + 1) * P], WA2[0:17, :], start=True, stop=True),
                     reads=['a17', 'wa2'], writes=psk(b_zs[tt]))

            def gla_chain(tt):
                ta, la = 2 * (tt % 2), 2 * (tt % 2) + 1
                b_z = b_zs[tt]
                S.op('act', lambda e: e.activation(TMP[:, ta, :], PSB[b_z][:, :], AF.Abs), reads=psk(b_z), writes=tk(ta))
                S.op('act', lambda e: e.activation(TMP[:, ta, :], TMP[:, ta, :], AF.Exp, scale=-1.0), reads=tk(ta), writes=tk(ta))
                S.op('act', lambda e: e.activation(TMP[:, ta, :], TMP[:, ta, :], AF.Ln, bias=1.0, scale=1.0), reads=tk(ta), writes=tk(ta))
                S.op('dve', lambda e: e.scalar_tensor_tensor(TMP[:, la, :], PSB[b_z][:, :], 0.0, TMP[:, ta, :], ALU.min, ALU.subtract),
                     reads=psk(b_z) + tk(ta), writes=tk(la))

            def gla_cum(tt):
                tsl = slice(tt * P, (tt + 1) * P)
                la = 2 * (tt % 2) + 1
                LA = TMP[:, la, :]
                b_k = ps_get('G')
                mm(PSB[b_k][:, :], [(Rb[:, kc, tsl], wK[:, kc, :]) for kc in range(8)], reads=[kK] + rb_keys(), writes=psk(b_k))
                b_b = ps_get('G')

                def fnb(e):
                    ins = None
                    for h in range(4):
                        ins = e.matmul(PSB[b_b][:, h * P:(h + 1) * P], LA[:, h * P:(h + 1) * P], UF, start=True, stop=True)
                    return ins
                S.op('pe', fnb, reads=tk(la) + ['cst'], writes=psk(b_b))
                b_r = ps_get('G')
                S.op('pe', lambda e: e.matmul(PSB[b_r][:, :], LF, LA, start=True, stop=True), reads=tk(la) + ['cst'], writes=psk(b_r))
                S.op('act', lambda e: e.activation(TMP[:, 4:8, tsl], PSB[b_b][:, :].rearrange("p (a b) -> p a b", a=4), AF.Exp),
                     reads=psk(b_b), writes=tk(4, 5, 6, 7))
                for jx in range(2):
                    S.op('act', lambda e, jx=jx: e.activation(
                        XIN[jx][:, :].rearrange("p (a b) -> p a b", a=2)[:, :, tsl],
                        PSB[b_b][:, jx * 256:(jx + 1) * 256].rearrange("p (a b) -> p a b", a=2), AF.Exp, scale=-1.0),
                        reads=psk(b_b), writes=[('xin', jx)])
                S.op('act', lambda e: e.activation(LA, PSB[b_r][:, :], AF.Exp), reads=psk(b_r), writes=tk(la))
                S.op('dve', lambda e: e.tensor_tensor(AR[:, 8 + tt, :], PSB[b_k][:, :], LA, ALU.mult),
                     reads=psk(b_k) + tk(la), writes=ark(8 + tt))

            gla_chain(0)
            gla_chain(1)
            gla_cum(0)
            gla_chain(2)
            gla_cum(1)
            gla_chain(3)
            gla_cum(2)
            gla_cum(3)
            for h in range(4):
                b = ps_get('A')
                proj(b, wK, kK, h * 128, 128, rb_aps(), rb_keys())
                S.op('dve', lambda e, b=b, h=h: e.tensor_tensor(AR[:, 4 + h, :], PSB[b][:, :], ENB(h), ALU.mult),
                     reads=psk(b) + [('xin', h // 2)], writes=ark(4 + h))
            wrel(iK)
            iQ, wQ, kQ = wget('BGQ')
            for h in range(4):
                b = ps_get('A')
                proj(b, wQ, kQ, h * 128, 128, rb_aps(), rb_keys())
                S.op('dve', lambda e, b=b, h=h: e.scalar_tensor_tensor(AR[:, h, :], PSB[b][:, :], GLA_QSCALE, EB(h), ALU.mult, ALU.mult),
                     reads=psk(b) + tk(4 + h), writes=ark(h))
            wrel(iQ)
            if g + 1 < ngroups:
                for tt in range(2):
                    S.dma('sp', XIN[tt][:, :], x_d[t0 + TG + tt * P:t0 + TG + (tt + 1) * P, :], writes=[('xin', tt)], sem=f'xin{tt}')
            for blk in range(2):
                iV, wV, kV = wget(f'BGV{blk}')
                for tt in range(4):
                    b = ps_get('A')
                    mm(PSB[b][:, :], [(Rb[:, kc, tt * P:(tt + 1) * P], wV[:, kc, :]) for kc in range(8)], reads=[kV] + rb_keys(), writes=psk(b))
                    evac_copy(AR[:, 12 + 2 * tt + blk, :], PSB[b][:, :], reads=psk(b), writes=ark(12 + 2 * tt + blk))
                wrel(iV)

            sm_pos = 0
            kv_pos = 0
            for pair in range(2):
                steps = [(tt, hl) for tt in range(4) for hl in range(2)]

                def emit_sT(tt, hl, pair=pair):
                    nonlocal sm_pos
                    h = pair * 2 + hl
                    tsl = slice(tt * P, (tt + 1) * P)
                    qd = sm_pos % 4
                    bsT = 4 + 2 * (sm_pos % 2)
                    sm_pos += 1
                    S.op('pe', lambda e: e.matmul(PSB[bsT][:, 0:P], AR[:, 4 + h, tsl], AR[:, h, tsl], start=True, stop=True),
                         reads=ark(4 + h, h), writes=psk(bsT))
                    S.op('dve', lambda e: e.tensor_tensor(SM[:, qd, :], PSB[bsT][:, 0:P], MU[:, :], ALU.mult),
                         reads=psk(bsT) + ['mu'], writes=[('sm', qd)])
                    return qd

                pend_qd = emit_sT(*steps[0])
                for si, (tt, hl) in enumerate(steps):
                    qd = pend_qd
                    if si + 1 < len(steps):
                        pend_qd = emit_sT(*steps[si + 1])
                    tsl = slice(tt * P, (tt + 1) * P)
                    h = pair * 2 + hl
                    vslot = 12 + 2 * tt + (h // 2)
                    vc0 = (h % 2) * 256
                    for dvc in range(2):
                        b_o = hl * 2 + dvc

                        def fno(e, h=h, tsl=tsl, qd=qd, dvc=dvc, b_o=b_o, vslot=vslot, vc0=vc0, tt=tt):
                            e.matmul(PSB[b_o][:, tsl], AR[:, vslot, vc0 + dvc * P:vc0 + (dvc + 1) * P], SM[:, qd, :], start=True, stop=False)
                            return e.matmul(PSB[b_o][:, tsl], Sbf[:, h, dvc * P:(dvc + 1) * P], AR[:, h, tsl], start=False, stop=True)
                        S.op('pe', fno, reads=ark(vslot, h) + [('sm', qd), ('Sbf', h)], writes=[('ps', b_o, tt)])
                    bkv = 5 + 2 * (kv_pos % 2)
                    kv_pos += 1
                    S.op('pe', lambda e, h=h, tt=tt, bkv=bkv, vslot=vslot, vc0=vc0: e.matmul(
                        PSB[bkv][:, 0:256], AR[:, 8 + tt, h * P:(h + 1) * P], AR[:, vslot, vc0:vc0 + 256], start=True, stop=True),
                        reads=ark(8 + tt, vslot), writes=psk(bkv))
                    S.op('dve', lambda e, h=h, tt=tt, bkv=bkv: e.scalar_tensor_tensor(
                        Sst[:, h, :], Sst[:, h, :], TMP[:, 4 + h, tt * P + P - 1:tt * P + P], PSB[bkv][:, 0:256], ALU.mult, ALU.add),
                        reads=[('Sst', h)] + psk(bkv) + tk(4 + h), writes=[('Sst', h)])
                    S.op('act', lambda e, h=h: e.activation(Sbf[:, h, :], Sst[:, h, :], AF.Copy), reads=[('Sst', h)], writes=[('Sbf', h)])
                iR, wR, kR = wget(f'BR{pair}')
                b_stat = 7
                rst = [1, 0]
                chunks = [(hl, dvc) for hl in range(2) for dvc in range(2)]

                def r_proj(ci):
                    hl, dvc = chunks[ci]
                    c = 2 * (pair * 2 + hl) + dvc
                    b_rr = 4 + (ci % 3)
                    proj(b_rr, wR, kR, (c % 4) * 128, 128, rb_aps(), rb_keys())
                    return b_rr
                brs = []
                for hl in range(2):
                    banks = [hl * 2, hl * 2 + 1]
                    sqs = []
                    for i, bq in enumerate(banks):
                        s2 = scr_get()
                        sqs.append(s2)
                        S.op('act', lambda e, bq=bq, s2=s2: e.activation(AR[:, s2, :], PSB[bq][:, :], AF.Square), reads=psk(bq), writes=ark(s2))
                    brs.append(r_proj(hl))
                    for i, s2 in enumerate(sqs):
                        S.op('pe', lambda e, i=i, s2=s2: e.matmul(PSB[b_stat][:, :], ONES[:, :], AR[:, s2, :], start=(i == 0), stop=(i == 1)),
                             reads=ark(s2) + ['ones'] + psk(b_stat), writes=psk(b_stat))
                    tr_ = rst[hl]
                    S.op('act', lambda e, tr_=tr_: e.activation(TMP[:, tr_, :], PSB[b_stat][:, :], AF.Ln, bias=RMS_EPS, scale=1.0 / 256), reads=psk(b_stat), writes=tk(tr_))
                    S.op('act', lambda e, tr_=tr_: e.activation(TMP[:, tr_, :], TMP[:, tr_, :], AF.Exp, scale=-0.5), reads=tk(tr_), writes=tk(tr_))
                brs.append(r_proj(2))
                for ci, (hl, dvc) in enumerate(chunks):
                    h = pair * 2 + hl
                    c = 2 * h + dvc
                    b_o = hl * 2 + dvc
                    tr_ = rst[hl]
                    ts_ = 2 + (c % 2)
                    if ci == 3:
                        brs.append(r_proj(3))
                    b_rr = brs[ci]
                    S.op('act', lambda e, b_rr=b_rr, ts_=ts_: e.activation(TMP[:, ts_, :], PSB[b_rr][:, :], AF.Silu), reads=psk(b_rr), writes=tk(ts_))
                    S.op('dve', lambda e, ts_=ts_, tr_=tr_: e.tensor_tensor(TMP[:, ts_, :], TMP[:, ts_, :], TMP[:, tr_, :], ALU.mult), reads=tk(ts_, tr_), writes=tk(ts_))
                    S.op('dve', lambda e, c=c, b_o=b_o, ts_=ts_: e.scalar_tensor_tensor(AR[:, 20 + c, :], PSB[b_o][:, :], pvc(PV_GLAG + c), TMP[:, ts_, :], ALU.mult, ALU.mult),
                         reads=psk(b_o) + tk(ts_) + ['pv'], writes=ark(20 + c))
                wrel(iR)

            out_proj_gated('WOG', 'BGG', list(range(20, 28)), 0, add_prev=True)

            S.labels.append(('7', S.seq))
            for hb in range(2):
                iW, wW, kW = wget(f'WOUT{hb}')
                for cl in range(4):
                    c = hb * 4 + cl
                    b = ps_get('A')
                    proj(b, wW, kW, cl * 128, 128, [AR[:, 28 + k, :] for k in range(8)], ark(*range(28, 36)))
                    S.op('dve', lambda e, b=b, c=c: e.scalar_tensor_tensor(R32[:, c, :], R32[:, c, :], ALPHA, PSB[b][:, :], ALU.mult, ALU.add),
                         reads=psk(b) + [('R32', c)], writes=[('R32', c)])
                wrel(iW)
            S.labels.append(('8', S.seq))
            layernorm(PV_LN1_G, PV_LN1_B)
            if g + 1 < ngroups:
                x_chain(0)
                x_chain(1)
            S.labels.append(('9', S.seq))
            for hf in range(2):
                for q in range(4):
                    iW, wW, kW = wget(f'FF1_{hf * 4 + q}')
                    fb = [ps_get('F') for _ in range(4)]
                    if hf == 0 and q == 0:
                        proj_kc_major(fb, wW, kW, [0, 128, 256, 384])
                    for jl in range(4):
                        j = q * 4 + jl
                        b = fb[jl]
                        if not (hf == 0 and q == 0):
                            proj(b, wW, kW, jl * 128, 128, rb_aps(), rb_keys())
                        tr = j % 4
                        S.op('act', lambda e, b=b, tr=tr: e.activation(TMP[:, tr, :], PSB[b][:, :], AF.Relu), reads=psk(b), writes=tk(tr))
                        S.op('dve', lambda e, tr=tr, j=j: e.tensor_tensor(AR[:, j, :], TMP[:, tr, :], TMP[:, tr, :], ALU.mult), reads=tk(tr), writes=ark(j))
                    wrel(iW)
                for q in range(4):
                    iW, wW, kW = wget(f'FF2_{hf * 4 + q}')
                    for cl in range(2):
                        c = q * 2 + cl
                        b = ps_get('C')
                        proj(b, wW, kW, cl * 128, 128, [AR[:, j, :] for j in range(16)], ark(*range(16)))
                        if hf == 0:
                            S.op('dve', lambda e, b=b, c=c: e.scalar_tensor_tensor(R32[:, c, :], R32[:, c, :], ALPHA, PSB[b][:, :], ALU.mult, ALU.add),
                                 reads=psk(b) + [('R32', c)], writes=[('R32', c)])
                        else:
                            S.op('dve', lambda e, b=b, c=c: e.tensor_tensor(R32[:, c, :], R32[:, c, :], PSB[b][:, :], ALU.add),
                                 reads=psk(b) + [('R32', c)], writes=[('R32', c)])
                    wrel(iW)
            S.labels.append(('10', S.seq))
            for tt in range(4):
                for half in range(2):
                    bb = 2 * tt + half

                    def fn(e, tt=tt, half=half, bb=bb):
                        ins = None
                        for q in range(4):
                            c = half * 4 + q
                            ins = e.transpose(PSB[bb][:, q * P:(q + 1) * P], R32[:, c, tt * P:(tt + 1) * P], IDENT)
                        return ins
                    S.op('pe', fn, reads=[('R32', half * 4 + q) for q in range(4)] + ['cst'], writes=psk(bb))
            for tt in range(4):
                if tt == 2:
                    yield 'tailA'
                xs = (g * 4 + tt) % 2
                stage = TMP[:, 4 + 2 * xs:6 + 2 * xs, :].rearrange("p a b -> p (a b)")
                skeys = tk(4 + 2 * xs, 5 + 2 * xs)
                lk = [('lst2', xs)]
                bks = [2 * tt, 2 * tt + 1]
                for half in range(2):
                    S.op('dve', lambda e, xs=xs, half=half, bb=bks[half]: e.bn_stats(LST2[:, xs, 6 * half:6 * half + 6], PSB[bb][:, :]), reads=psk(bks[half]) + lk, writes=lk)
                S.op('dve', lambda e, xs=xs: e.bn_aggr(LST2[:, xs, 12:14], LST2[:, xs, 0:12]), reads=lk, writes=lk)
                S.op('act', lambda e, xs=xs: e.activation(LST2[:, xs, 14:15], LST2[:, xs, 13:14], AF.Ln, bias=LN_EPS, scale=1.0), reads=lk, writes=lk)
                S.op('act', lambda e, xs=xs: e.activation(LST2[:, xs, 14:15], LST2[:, xs, 14:15], AF.Exp, scale=-0.5), reads=lk, writes=lk)
                S.op('dve', lambda e, xs=xs: e.scalar_tensor_tensor(LST2[:, xs, 15:16], LST2[:, xs, 12:13], -1.0, LST2[:, xs, 14:15], ALU.mult, ALU.mult), reads=lk, writes=lk)
                for half in range(2):
                    S.op('act', lambda e, xs=xs, half=half, bb=bks[half], stage=stage: e.activation(
                        stage[:, half * 512:(half + 1) * 512], PSB[bb][:, :], AF.Identity, bias=LST2[:, xs, 15:16], scale=LST2[:, xs, 14:15]),
                        reads=psk(bks[half]) + lk, writes=[skeys[half]])
                S.op('dve', lambda e, stage=stage: e.tensor_tensor(stage, stage, G2B2[:, 0, :], ALU.mult), reads=skeys + ['g2'], writes=skeys)
                S.op('dve', lambda e, stage=stage: e.tensor_tensor(stage, stage, G2B2[:, 1, :], ALU.add), reads=skeys + ['b2'], writes=skeys)
                S.dma('pool', out_d[t0 + tt * P:t0 + (tt + 1) * P, :], stage, reads=skeys, sem=f'out{xs}', final=True)

        gens = [do_group(g) for g in range(ngroups)]
        next(gens[0])
        for g in range(ngroups):
            next(gens[g])
            if g + 1 < ngroups:
                next(gens[g + 1])
            for _ in gens[g]:
                pass
        if dbg:
            dbg[0](S, locals(), dbg_d)
        S.emit()
    global LAST_SCHED
    LAST_SCHED = S
    return nc


def _blk(Wm):
    K, n = Wm.shape
    kc = K // P
    return Wm.reshape(kc, P, n).transpose(1, 0, 2).reshape(P, kc * n)


def _col(v):
    v = np.asarray(v, np.float32).reshape(-1)
    return v.reshape(-1, P).T


def prepare_shared(inp):
    w_in = np.asarray(inp['w_in'][0], np.float32)
    w_q_b = np.asarray(inp['w_q_b'][0], np.float32)
    w_kv_b = np.asarray(inp['w_kv_b'][0], np.float32)
    w_o_gla = np.asarray(inp['w_o_gla'][0], np.float32)
    w_o_mla = np.asarray(inp['w_o_mla'][0], np.float32)
    w_out = np.asarray(inp['w_out'][0], np.float32)
    w_ff1 = np.asarray(inp['w_ff1'][0], np.float32)
    w_ff2 = np.asarray(inp['w_ff2'][0], np.float32)
    perm = np.concatenate([np.arange(32, 64), np.arange(0, 32)])
    kr = w_in[:, 3728:3792]
    krs = kr[:, perm]
    GATE0 = 3792
    mats = {}
    mats['BA'] = np.concatenate([kr, kr, krs, krs, w_in[:, 3472:3728]], axis=1)
    mats['BQ'] = np.concatenate([w_in[:, 3088:3472], w_in[:, 3072:3088]], axis=1)
    for hb in range(2):
        hs = range(hb * 4, hb * 4 + 4)
        nope = [w_q_b[:, h * 192:h * 192 + 128] for h in hs]
        rope = [w_q_b[:, h * 192 + 128:(h + 1) * 192] for h in hs]
        sw = [r[:, perm] for r in rope]
        mats[f'WQB{hb}'] = np.concatenate(nope + rope + sw, axis=1)
    mats['KVK'] = np.concatenate([w_kv_b[:, h * 256:h * 256 + 128] for h in range(8)], axis=1)
    mats['KVV'] = np.concatenate([w_kv_b[:, h * 256 + 128:(h + 1) * 256] for h in range(8)], axis=1)
    for hb in range(2):
        cs = slice(hb * 512, (hb + 1) * 512)
        mats[f'WOM{hb}'] = w_o_mla[:, cs]
        mats[f'BGM{hb}'] = w_in[:, GATE0 + 1024 + hb * 512:GATE0 + 1024 + (hb + 1) * 512]
        mats[f'WOG{hb}'] = w_o_gla[:, cs]
        mats[f'BGG{hb}'] = w_in[:, GATE0 + hb * 512:GATE0 + (hb + 1) * 512]
        mats[f'WOUT{hb}'] = w_out[:, cs]
        mats[f'BGV{hb}'] = w_in[:, 1024 + hb * 512:1024 + (hb + 1) * 512]
        mats[f'BR{hb}'] = w_in[:, 2048 + hb * 512:2048 + (hb + 1) * 512]
    mats['BGK'] = w_in[:, 512:1024]
    mats['BGQ'] = w_in[:, 0:512]
    for i in range(8):
        mats[f'FF1_{i}'] = w_ff1[:, i * 512:(i + 1) * 512]
        hf, q = i // 4, i % 4
        mats[f'FF2_{i}'] = w_ff2[hf * 2048:(hf + 1) * 2048, q * 256:(q + 1) * 256]
    ws = np.empty((P, WTOT), np.float32)
    for (name, kc, n), off in zip(BLOCKS, BLK_OFF):
        m = mats[name]
        assert m.shape == (kc * P, n), (name, m.shape)
        ws[:, off:off + kc * n] = _blk(m)
    pv = np.zeros((P, NPV), np.float32)
    pv[:, PV_LNIN_G:PV_LNIN_G + 8] = _col(inp['ln_in_g'])
    pv[:, PV_LNIN_B:PV_LNIN_B + 8] = _col(inp['ln_in_b'])
    pv[:, PV_LN1_G:PV_LN1_G + 8] = _col(inp['ln1_g'])
    pv[:, PV_LN1_B:PV_LN1_B + 8] = _col(inp['ln1_b'])
    pv[:, PV_LN2_G:PV_LN2_G + 8] = _col(inp['ln2_g'])
    pv[:, PV_LN2_B:PV_LN2_B + 8] = _col(inp['ln2_b'])
    pv[:, PV_BGATE:PV_BGATE + 16] = _col(inp['b_gate'])
    pv[:, PV_GLAG:PV_GLAG + 8] = _col(inp['gla_norm_g'])
    pv[:, PV_QG:PV_QG + 3] = _col(inp['q_a_norm_g'])
    pv[:, PV_KVG:PV_KVG + 2] = _col(inp['kv_a_norm_g'])
    inv_freq = (1.0 / (10000.0 ** (np.arange(0, 64, 2, dtype=np.float32) / np.float32(64)))).astype(np.float32)
    pidx = np.arange(P)
    pv[:, PV_INVF] = inv_freq[pidx % 32]
    pv[:, PV_SGN] = np.where((pidx % 64) < 32, -1.0, 1.0)
    wa2 = np.concatenate([np.asarray(inp['w_gla_a2'][0], np.float32), np.asarray(inp['b_gla_a2'], np.float32).reshape(1, 512)], axis=0)
    jj, ii = np.meshgrid(np.arange(P), np.arange(P), indexing='ij')
    cst = np.concatenate([np.eye(P, dtype=np.float32), (jj <= ii).astype(np.float32) / 16.0, (jj > ii).astype(np.float32) / 16.0], axis=1)
    lnout = np.ascontiguousarray(np.stack([np.asarray(inp['ln2_g'], np.float32).reshape(-1), np.asarray(inp['ln2_b'], np.float32).reshape(-1)], axis=0))
    return dict(wstream=ws, pvec=pv, wa2=np.ascontiguousarray(wa2), cst=np.ascontiguousarray(cst), lnout=lnout)


_NC_CACHE = {}
LAST_SCHED = None


def kernel(**inputs):
    x = np.asarray(inputs['x'], np.float32)
    pos = np.asarray(inputs['positions'], np.int32)
    shared = prepare_shared(inputs)
    if 'nc' not in _NC_CACHE:
        _NC_CACHE['nc'] = build_program()
    nc = _NC_CACHE['nc']
    in_maps = []
    for b in range(8):
        m = dict(shared)
        m['x'] = np.ascontiguousarray(x[b])
        m['pos'] = np.ascontiguousarray(pos[b].reshape(1, SEQ))
        in_maps.append(m)
    res = run_bass_kernel_spmd(nc, in_maps, core_ids=list(range(8)))
    return np.stack([np.asarray(r['out'], np.float32) for r in res.results], axis=0)
```

```python
import numpy as np
from contextlib import ExitStack
import concourse.bass as bass
import concourse.mybir as mybir
from concourse.bass_utils import run_bass_kernel_spmd

F32 = mybir.dt.float32
BF16 = mybir.dt.bfloat16
I32 = mybir.dt.int32
AF = mybir.ActivationFunctionType
ALU = mybir.AluOpType


class Sched:
    ENGS = ('pe', 'act', 'dve', 'pool', 'sp')

    def __init__(self, nc):
        self.nc = nc
        self.ops = {e: [] for e in self.ENGS}
        self.count = {e: 0 for e in self.ENGS}
        self.waited = {e: {} for e in self.ENGS}
        self.last_write = {}
        self.readers = {}
        self.dma_count = {}
        self.final_tokens = []
        self.seq = 0
        self.cut = None
        self.labels = []
        self.bank_last = {}

    def _deps(self, eng, reads, writes):
        deps = {}

        def add(tok):
            if tok is None:
                return
            s, v = tok
            if deps.get(s, 0) < v:
                deps[s] = v
        for k in reads:
            add(self.last_write.get(k))
        for k in writes:
            add(self.last_write.get(k))
            for r in self.readers.get(k, ()):
                add(r)
        for k in list(reads) + list(writes):
            if isinstance(k, tuple) and k[0] == 'ps':
                for e2, tok in self.bank_last.get(k[1], {}).items():
                    if e2 != eng:
                        add(tok)
        waits = []
        for s, v in deps.items():
            if eng == 'pe' and s == 'S_pe':
                continue
            if self.waited[eng].get(s, 0) >= v:
                continue
            self.waited[eng][s] = v
            waits.append((s, v))
        return waits

    def _commit(self, tok, reads, writes, eng=None):
        for k in list(reads) + list(writes):
            if isinstance(k, tuple) and k[0] == 'ps':
                self.bank_last.setdefault(k[1], {})[eng] = tok
        for k in reads:
            self.readers.setdefault(k, []).append(tok)
        for k in writes:
            self.last_write[k] = tok
            self.readers[k] = []

    def op(self, eng, fn, reads=(), writes=()):
        waits = self._deps(eng, reads, writes)
        self.count[eng] += 1
        tok = ('S_' + eng, self.count[eng])
        self.ops[eng].append((waits, fn, tok[0], 1, self.seq))
        self.seq += 1
        self._commit(tok, reads, writes, eng)
        return tok

    def dma(self, eng, out, in_, reads=(), writes=(), sem=None, final=False):
        waits = self._deps(eng, reads, writes)
        sname = 'D_' + sem
        self.dma_count[sname] = self.dma_count.get(sname, 0) + 16
        tok = (sname, self.dma_count[sname])
        self.ops[eng].append((waits, lambda e: e.dma_start(out=out, in_=in_), sname, 16, self.seq))
        self.seq += 1
        self._commit(tok, reads, writes)
        if final:
            self.final_tokens.append(tok + (self.seq - 1,))
        return tok

    def emit(self):
        nc = self.nc
        names = ['S_' + e for e in self.ENGS if e != 'sp'] + sorted(self.dma_count)
        with ExitStack() as st:
            sems = {n: st.enter_context(nc.semaphore(n)) for n in names}
            block = st.enter_context(nc.Block())

            def run(e, h):
                for waits, fn, sname, inc, seq in self.ops[e]:
                    if self.cut is not None and seq >= self.cut:
                        break
                    for s, v in waits:
                        h.wait_ge(sems[s], v)
                    fn(h).then_inc(sems[sname], inc)
                if e == 'sp':
                    fin = {}
                    for e2 in self.ENGS:
                        for waits, fn, sname, inc, seq in self.ops[e2]:
                            if inc == 16 and (self.cut is None or seq < self.cut):
                                fin[sname] = fin.get(sname, 0) + 16
                    for s, v, seq in self.final_tokens:
                        if self.cut is not None and seq >= self.cut:
                            continue
                        fin[s] = max(fin.get(s, 0), v)
                    for s, v in fin.items():
                        h.wait_ge(sems[s], v)

            @block.tensor
            def _(h):
                run('pe', h)

            @block.scalar
            def _(h):
                run('act', h)

            @block.vector
            def _(h):
                run('dve', h)

            @block.gpsimd
            def _(h):
                run('pool', h)

            @block.sync
            def _(h):
                run('sp', h)


P = 128
TG = 512
NG = 4
SEQ = 2048
D = 1024
NSLOT = 4
WELEMS = 4096
NPV = 80
ALPHA = 2.0 ** 0.25
LN_EPS = 1e-5
RMS_EPS = 1e-6
ATT_SCALE = 192.0 ** -0.5
GLA_QSCALE = 128.0 ** -0.5
TWO_PI = 2.0 * np.pi
CW1 = 6.28125
CW2 = TWO_PI - CW1

PV_LNIN_G, PV_LNIN_B, PV_LN1_G, PV_LN1_B, PV_LN2_G, PV_LN2_B = 0, 8, 16, 24, 32, 40
PV_BGATE, PV_GLAG, PV_QG, PV_KVG, PV_INVF, PV_SGN = 48, 64, 72, 75, 77, 78

BLOCKS = [
    ('BA', 8, 512), ('BQ', 8, 400), ('KVK', 2, 1024), ('KVV', 2, 1024),
    ('WQB0', 3, 1024), ('WQB1', 3, 1024),
    ('WOM0', 8, 512), ('BGM0', 8, 512), ('WOM1', 8, 512), ('BGM1', 8, 512),
    ('BGK', 8, 512), ('BGQ', 8, 512), ('BGV0', 8, 512), ('BGV1', 8, 512),
    ('BR0', 8, 512), ('BR1', 8, 512),
    ('WOG0', 8, 512), ('BGG0', 8, 512), ('WOG1', 8, 512), ('BGG1', 8, 512),
    ('WOUT0', 8, 512), ('WOUT1', 8, 512),
    ('FF1_0', 8, 512), ('FF1_1', 8, 512), ('FF1_2', 8, 512), ('FF1_3', 8, 512),
    ('FF2_0', 16, 256), ('FF2_1', 16, 256), ('FF2_2', 16, 256), ('FF2_3', 16, 256),
    ('FF1_4', 8, 512), ('FF1_5', 8, 512), ('FF1_6', 8, 512), ('FF1_7', 8, 512),
    ('FF2_4', 16, 256), ('FF2_5', 16, 256), ('FF2_6', 16, 256), ('FF2_7', 16, 256),
]
BLK_OFF = []
_o = 0
for _n, _kc, _nn in BLOCKS:
    BLK_OFF.append(_o)
    _o += _kc * _nn
WTOT = _o


def build_program(dbg=None, ngroups=NG, cut=None, marks=None):
    nc = bass.Bass("TRN2", target_bir_lowering=False)
    x_d = nc.dram_tensor("x", [SEQ, D], F32, kind="ExternalInput").ap()
    pos_d = nc.dram_tensor("pos", [1, SEQ], I32, kind="ExternalInput").ap()
    ws_d = nc.dram_tensor("wstream", [P, WTOT], F32, kind="ExternalInput").ap()
    pv_d = nc.dram_tensor("pvec", [P, NPV], F32, kind="ExternalInput").ap()
    wa2_d = nc.dram_tensor("wa2", [17, 512], F32, kind="ExternalInput").ap()
    cst_d = nc.dram_tensor("cst", [P, 384], F32, kind="ExternalInput").ap()
    lno_d = nc.dram_tensor("lnout", [2, D], F32, kind="ExternalInput").ap()
    out_d = nc.dram_tensor("out", [SEQ, D], F32, kind="ExternalOutput").ap()
    if dbg:
        dbg_d = nc.dram_tensor("dbg", [P, dbg[1]], F32, kind="ExternalOutput").ap()

    with ExitStack() as st:
        def sb(name, shape, dt):
            return st.enter_context(nc.sbuf_tensor(name, shape, dt))

        KT = sb("KT", [P, 8, SEQ], BF16)
        KR = sb("KR", [P, SEQ], BF16)
        V = sb("V", [P, 16, 1024], BF16)
        Sst = sb("Sst", [P, 4, 256], F32)
        Sbf = sb("Sbf", [P, 4, 256], BF16)
        WS = [sb(f"ws{i}", [P, WELEMS], BF16) for i in range(NSLOT)]
        CST = sb("CST", [P, 384], F32)
        MU = sb("MU", [P, 128], BF16)
        ONES = sb("ONES", [P, 128], BF16)
        PV = sb("PV", [P, NPV], F32)
        WA2 = sb("WA2", [17, 512], F32)
        A17 = sb("A17", [17, 512], F32)
        NBG = sb("NBG", [P, 16], F32)
        G2B2 = sb("G2B2", [P, 2, D], F32)
        LST2 = sb("LST2", [P, 2, 16], F32)
        LST = sb("LST", [P, 2, 16], F32)
        R32 = sb("R32", [P, 8, TG], F32)
        Rb = sb("Rb", [P, 8, TG], BF16)
        XIN = [sb(f"xin{i}", [P, D], F32) for i in range(2)]
        NS = 38
        AR = sb("AR", [P, NS, TG], BF16)
        NT = 8
        TMP = sb("TMP", [P, NT, TG], F32)
        SM = sb("SM", [P, 4, 128], BF16)
        PSB = [st.enter_context(nc.psum_tensor(f"psb{i}", [P, 512], F32)) for i in range(8)]

        POSB = TMP[:, 7, :].bitcast(I32)
        IDENT = CST[:, 0:128]
        UF = CST[:, 128:256]
        LF = CST[:, 256:384]

        S = Sched(nc)
        S.cut = cut

        def psk(b):
            return [('ps', b, q) for q in range(4)]

        def ark(*idx):
            return [('ar', i) for i in idx]

        def tk(*idx):
            return [('tmp', i) for i in idx]

        def pvc(col):
            return PV[:, col:col + 1]

        pools = {'A': [0, 1, 2, 3], 'S': [0, 1, 2, 3], 'B': [4, 5, 6, 7], 'C': [6, 7], 'G': [0, 1, 2, 3, 4, 5, 6, 7], 'F': [0, 1, 2, 3, 4, 5]}
        pool_pos = {k: 0 for k in pools}

        def ps_get(pool):
            b = pools[pool][pool_pos[pool] % len(pools[pool])]
            pool_pos[pool] += 1
            return b

        scr_pos = [0]

        def scr_get():
            i = 36 + (scr_pos[0] % 2)
            scr_pos[0] += 1
            return i

        ln_scr = [16, 17, 18, 19, 36, 37]
        ln_pos = [0]

        def ln_scr_get():
            i = ln_scr[ln_pos[0] % len(ln_scr)]
            ln_pos[0] += 1
            return i

        evac_rr = [0]

        def evac_copy(out_ap, in_ap, reads, writes, eng=None):
            if eng is None:
                eng = 'act' if evac_rr[0] % 2 == 0 else 'dve'
                evac_rr[0] += 1
            if eng == 'act':
                S.op('act', lambda e: e.activation(out_ap, in_ap, AF.Copy), reads, writes)
            else:
                S.op('dve', lambda e: e.tensor_copy(out_ap, in_ap), reads, writes)

        def mm(ps_ap, pairs, reads, writes):
            def fn(e):
                n = len(pairs)
                ins = None
                for i, (l, r) in enumerate(pairs):
                    ins = e.matmul(ps_ap, l, r, start=(i == 0), stop=(i == n - 1))
                return ins
            S.op('pe', fn, reads, writes)

        NB = len(BLOCKS)
        total_blocks = ngroups * NB
        wstate = {'next_dma': 0, 'next_use': 0}
        released = [False] * total_blocks

        def w_issue():
            while wstate['next_dma'] < total_blocks and (
                    wstate['next_dma'] < NSLOT or released[wstate['next_dma'] - NSLOT]):
                i = wstate['next_dma']
                name, kc, n = BLOCKS[i % NB]
                off = BLK_OFF[i % NB]
                s = i % NSLOT
                S.dma('pool', WS[s][:, 0:kc * n], ws_d[:, off:off + kc * n],
                      reads=([('xin', 0), ('xin', 1), 'cst', 'pv'] if i == 0 else []), writes=[('w', s)], sem=f'w{s}')
                wstate['next_dma'] += 1

        def wget(name):
            i = wstate['next_use']
            bname, kc, n = BLOCKS[i % NB]
            assert bname == name, (bname, name)
            wstate['next_use'] += 1
            s = i % NSLOT
            view = WS[s][:, 0:kc * n].rearrange("p (k n) -> p k n", k=kc)
            return i, view, ('w', s)

        def wrel(i):
            released[i] = True
            w_issue()

        for tt0 in range(2):
            S.dma('sp', XIN[tt0][:, :], x_d[tt0 * P:(tt0 + 1) * P, :], writes=[('xin', tt0)], sem=f'xin{tt0}')
        S.dma('sp', CST[:, :], cst_d[:, :], writes=['cst'], sem='cst')
        S.dma('sp', PV[:, :], pv_d[:, :], writes=['pv'], sem='pv')
        S.dma('sp', WA2[:, :], wa2_d[:, :], writes=['wa2'], sem='wa2')
        S.dma('sp', G2B2[:, 0, :], lno_d[0:1, :].partition_broadcast(P), writes=['g2'], sem='g2')
        S.dma('sp', G2B2[:, 1, :], lno_d[1:2, :].partition_broadcast(P), writes=['b2'], sem='b2')
        S.op('dve', lambda e: e.memset(ONES[:, :], 1.0), writes=['ones'])
        S.op('dve', lambda e: e.memset(A17[:, :], 1.0), writes=['a17'])
        S.op('dve', lambda e: e.memset(Sst[:, :, :], 0.0), writes=[('Sst', h) for h in range(4)])
        S.op('dve', lambda e: e.memset(Sbf[:, :, :], 0.0), writes=[('Sbf', h) for h in range(4)])
        S.op('dve', lambda e: e.tensor_copy(MU[:, :], UF), reads=['cst'], writes=['mu'])
        S.op('dve', lambda e: e.tensor_scalar(MU[:, :], MU[:, :], 16.0, None, ALU.mult), reads=['mu'], writes=['mu'])
        S.op('dve', lambda e: e.tensor_scalar(NBG[:, :], PV[:, PV_BGATE:PV_BGATE + 16], -1.0, None, ALU.mult), reads=['pv'], writes=['nbg'])
        w_issue()

        def layernorm(gcol, bcol, want_bf16=True):
            b_sum, b_sq = 6, 7
            for c in range(8):
                s1, s2 = ln_scr_get(), ln_scr_get()
                S.op('act', lambda e, c=c, s1=s1: e.activation(AR[:, s1, :], R32[:, c, :], AF.Copy),
                     reads=[('R32', c)], writes=ark(s1))
                S.op('act', lambda e, c=c, s2=s2: e.activation(AR[:, s2, :], R32[:, c, :], AF.Square),
                     reads=[('R32', c)], writes=ark(s2))
                S.op('pe', lambda e, c=c, s1=s1: e.matmul(PSB[b_sum][:, :], ONES[:, :], AR[:, s1, :], start=(c == 0), stop=(c == 7)),
                     reads=ark(s1) + ['ones'] + psk(b_sum), writes=psk(b_sum))
                S.op('pe', lambda e, c=c, s2=s2: e.matmul(PSB[b_sq][:, :], ONES[:, :], AR[:, s2, :], start=(c == 0), stop=(c == 7)),
                     reads=ark(s2) + ['ones'] + psk(b_sq), writes=psk(b_sq))
            T_mean, T_r, T_m2 = TMP[:, 0, :], TMP[:, 1, :], TMP[:, 2, :]
            S.op('dve', lambda e: e.tensor_scalar(T_mean, PSB[b_sum][:, :], 1.0 / D, None, ALU.mult),
                 reads=psk(b_sum), writes=tk(0))
            S.op('dve', lambda e: e.tensor_tensor(T_m2, T_mean, T_mean, ALU.mult), reads=tk(0), writes=tk(2))
            S.op('dve', lambda e: e.scalar_tensor_tensor(T_r, PSB[b_sq][:, :], 1.0 / D, T_m2, ALU.mult, ALU.subtract),
                 reads=psk(b_sq) + tk(2), writes=tk(1))
            S.op('act', lambda e: e.activation(T_r, T_r, AF.Ln, bias=LN_EPS, scale=1.0), reads=tk(1), writes=tk(1))
            S.op('act', lambda e: e.activation(T_r, T_r, AF.Exp, scale=-0.5), reads=tk(1), writes=tk(1))
            S.op('dve', lambda e: e.scalar_tensor_tensor(T_m2, T_mean, -1.0, T_r, ALU.mult, ALU.mult), reads=tk(0, 1), writes=tk(2))
            for c in range(8):
                S.op('dve', lambda e, c=c: e.tensor_tensor(R32[:, c, :], R32[:, c, :], T_r, ALU.mult),
                     reads=[('R32', c)] + tk(1), writes=[('R32', c)])
                S.op('dve', lambda e, c=c: e.tensor_tensor(R32[:, c, :], R32[:, c, :], T_m2, ALU.add),
                     reads=[('R32', c)] + tk(2), writes=[('R32', c)])
                if want_bf16:
                    S.op('act', lambda e, c=c: e.activation(Rb[:, c, :], R32[:, c, :], AF.Identity, bias=pvc(bcol + c), scale=pvc(gcol + c)),
                         reads=[('R32', c), 'pv'], writes=[('Rb', c)])
            for c in range(8):
                S.op('act', lambda e, c=c: e.activation(R32[:, c, :], R32[:, c, :], AF.Identity, bias=pvc(bcol + c), scale=pvc(gcol + c)),
                     reads=[('R32', c), 'pv'], writes=[('R32', c)])

        def proj(bank, wv, wkey, col0, m, rhs_aps, rhs_keys, prow=P):
            pairs = [(wv[:, kc, col0:col0 + m], rhs_aps[kc]) for kc in range(len(rhs_aps))]
            mm(PSB[bank][0:m, :], pairs, reads=[wkey] + rhs_keys, writes=psk(bank))

        def proj_kc_major(banks, wv, wkey, col0s, m=128):
            for kc in range(8):
                for bank, col0 in zip(banks, col0s):
                    S.op('pe', lambda e, bank=bank, col0=col0, kc=kc: e.matmul(PSB[bank][0:m, :], wv[:, kc, col0:col0 + m], Rb[:, kc, :],
                                                                              start=(kc == 0), stop=(kc == 7)),
                         reads=[wkey, ('Rb', kc)] + psk(bank), writes=psk(bank))

        def x_chain(xs):
            lk = [('lst', xs)]
            S.op('dve', lambda e: e.bn_stats(LST[:, xs, 0:6], XIN[xs][:, 0:512]), reads=[('xin', xs)], writes=lk)
            S.op('dve', lambda e: e.bn_stats(LST[:, xs, 6:12], XIN[xs][:, 512:1024]), reads=[('xin', xs)] + lk, writes=lk)
            S.op('dve', lambda e: e.bn_aggr(LST[:, xs, 12:14], LST[:, xs, 0:12]), reads=lk, writes=lk)
            S.op('act', lambda e: e.activation(LST[:, xs, 14:15], LST[:, xs, 13:14], AF.Ln, bias=LN_EPS, scale=1.0), reads=lk, writes=lk)
            S.op('act', lambda e: e.activation(LST[:, xs, 14:15], LST[:, xs, 14:15], AF.Exp, scale=-0.5), reads=lk, writes=lk)
            S.op('dve', lambda e: e.scalar_tensor_tensor(LST[:, xs, 15:16], LST[:, xs, 12:13], -1.0, LST[:, xs, 14:15], ALU.mult, ALU.mult), reads=lk, writes=lk)
            S.op('act', lambda e: e.activation(XIN[xs][:, :], XIN[xs][:, :], AF.Identity, bias=LST[:, xs, 15:16], scale=LST[:, xs, 14:15]),
                 reads=[('xin', xs)] + lk, writes=[('xin', xs)])

        def rb_aps():
            return [Rb[:, kc, :] for kc in range(8)]

        def rb_keys():
            return [('Rb', kc) for kc in range(8)]

        def rms_rstd(ps_banks, nfeat, t_out):
            b_stat = 7
            n = len(ps_banks)
            for i, b in enumerate(ps_banks):
                s2 = scr_get()
                S.op('act', lambda e, b=b, s2=s2: e.activation(AR[:, s2, :], PSB[b][:, :], AF.Square),
                     reads=psk(b), writes=ark(s2))
                S.op('pe', lambda e, i=i, s2=s2: e.matmul(PSB[b_stat][:, :], ONES[:, :], AR[:, s2, :], start=(i == 0), stop=(i == n - 1)),
                     reads=ark(s2) + ['ones'] + psk(b_stat), writes=psk(b_stat))
            T = TMP[:, t_out, :]
            S.op('act', lambda e: e.activation(T, PSB[b_stat][:, :], AF.Ln, bias=RMS_EPS, scale=1.0 / nfeat),
                 reads=psk(b_stat), writes=tk(t_out))
            S.op('act', lambda e: e.activation(T, T, AF.Exp, scale=-0.5), reads=tk(t_out), writes=tk(t_out))

        def do_group(g):
            t0 = g * TG
            S.labels.append(('1', S.seq))
            for tt in range(4):
                xs = (g * 4 + tt) % 2
                if tt >= 2:
                    S.dma('sp', XIN[xs][:, :], x_d[t0 + tt * P:t0 + (tt + 1) * P, :], writes=[('xin', xs)], sem=f'xin{xs}')
                if g == 0 or tt >= 2:
                    x_chain(xs)
                for half in range(2):
                    b = ps_get('A')

                    def fn(e, xs=xs, half=half, b=b):
                        ins = None
                        for q in range(4):
                            c = half * 4 + q
                            ins = e.transpose(PSB[b][:, q * P:(q + 1) * P], XIN[xs][:, c * P:(c + 1) * P], IDENT)
                        return ins
                    S.op('pe', fn, reads=[('xin', xs), 'cst'], writes=psk(b))
                    evac_copy(R32[:, half * 4:half * 4 + 4, tt * P:(tt + 1) * P],
                              PSB[b][:, :].rearrange("p (a b) -> p a b", a=4),
                              reads=psk(b), writes=[('R32', half * 4 + q) for q in range(4)])
                if tt == 1:
                    yield 'headA'
            S.labels.append(('2', S.seq))
            for c in range(8):
                if c % 2 == 0:
                    S.op('act', lambda e, c=c: e.activation(Rb[:, c, :], R32[:, c, :], AF.Identity, bias=pvc(PV_LNIN_B + c), scale=pvc(PV_LNIN_G + c)),
                         reads=[('R32', c), 'pv'], writes=[('Rb', c)])
                else:
                    S.op('dve', lambda e, c=c: e.tensor_scalar(Rb[:, c, :], R32[:, c, :], pvc(PV_LNIN_G + c), pvc(PV_LNIN_B + c), ALU.mult, ALU.add),
                         reads=[('R32', c), 'pv'], writes=[('Rb', c)])

            def deferred_r32_affine():
                for c in range(8):
                    S.op('dve', lambda e, c=c: e.tensor_scalar(R32[:, c, :], R32[:, c, :], pvc(PV_LNIN_G + c), pvc(PV_LNIN_B + c), ALU.mult, ALU.add),
                         reads=[('R32', c), 'pv'], writes=[('R32', c)])

            S.labels.append(('0', S.seq))
            S.dma('sp', POSB, pos_d[0:1, t0:t0 + TG].partition_broadcast(P), writes=tk(7), sem='pos')
            T_ang, T_k, T_c, T_s = TMP[:, 5, :], TMP[:, 6, :], TMP[:, 3, :], TMP[:, 4, :]
            KI = POSB
            S.op('dve', lambda e: e.tensor_copy(T_ang, POSB), reads=tk(7), writes=tk(5))
            S.op('dve', lambda e: e.tensor_scalar(T_ang, T_ang, pvc(PV_INVF), None, ALU.mult), reads=tk(5) + ['pv'], writes=tk(5))
            S.op('dve', lambda e: e.tensor_scalar(T_k, T_ang, 1.0 / TWO_PI, None, ALU.mult), reads=tk(5), writes=tk(6))
            S.op('dve', lambda e: e.tensor_copy(KI, T_k), reads=tk(6), writes=tk(7))
            S.op('dve', lambda e: e.tensor_copy(T_k, KI), reads=tk(7), writes=tk(6))
            S.op('dve', lambda e: e.scalar_tensor_tensor(T_ang, T_k, -CW1, T_ang, ALU.mult, ALU.add), reads=tk(5, 6), writes=tk(5))
            S.op('dve', lambda e: e.scalar_tensor_tensor(T_ang, T_k, -CW2, T_ang, ALU.mult, ALU.add), reads=tk(5, 6), writes=tk(5))
            S.op('dve', lambda e: e.tensor_scalar(T_ang, T_ang, float(np.pi), float(-np.pi), ALU.min, ALU.max), reads=tk(5), writes=tk(5))
            S.op('act', lambda e: e.activation(T_s, T_ang, AF.Sin), reads=tk(5), writes=tk(4))
            S.op('dve', lambda e: e.tensor_scalar(T_s, T_s, pvc(PV_SGN), None, ALU.mult), reads=tk(4) + ['pv'], writes=tk(4))
            S.op('dve', lambda e: e.scalar_tensor_tensor(T_k, T_ang, -1.0, T_ang, ALU.mult, ALU.max), reads=tk(5), writes=tk(6))
            S.op('act', lambda e: e.activation(T_c, T_k, AF.Sin, bias=float(np.pi / 2), scale=-1.0), reads=tk(6), writes=tk(3))

            S.labels.append(('3', S.seq))
            iBA, wBA, kBA = wget('BA')
            b_kr, b_krs = ps_get('F'), ps_get('F')
            b_c = [ps_get('F'), ps_get('F')]
            proj_kc_major([b_kr, b_krs, b_c[0], b_c[1]], wBA, kBA, [0, 128, 256, 384])
            S.op('dve', lambda e: e.tensor_tensor(TMP[:, 5, :], PSB[b_kr][:, :], T_c, ALU.mult), reads=psk(b_kr) + tk(3), writes=tk(5))
            S.op('dve', lambda e: e.tensor_tensor(TMP[:, 6, :], PSB[b_krs][:, :], T_s, ALU.mult), reads=psk(b_krs) + tk(4), writes=tk(6))
            S.op('dve', lambda e: e.tensor_tensor(KR[:, t0:t0 + TG], TMP[:, 5, :], TMP[:, 6, :], ALU.add), reads=tk(5, 6), writes=[('KR', g)])
            wrel(iBA)
            CK32 = [XIN[0][:, 0:512], XIN[0][:, 512:1024]]
            QL32 = [XIN[1][:, 0:512], XIN[1][:, 512:1024], TMP[:, 7, :]]
            QLK = [[('xin', 1)], [('xin', 1)], tk(7)]
            for i in range(2):
                evac_copy(CK32[i], PSB[b_c[i]][:, :], reads=psk(b_c[i]), writes=[('xin', 0)], eng='act')
            rms_rstd(b_c, 256, 1)
            for i in range(2):
                S.op('dve', lambda e, i=i: e.scalar_tensor_tensor(AR[:, i, :], CK32[i], pvc(PV_KVG + i), TMP[:, 1, :], ALU.mult, ALU.mult),
                     reads=tk(1) + [('xin', 0), 'pv'], writes=ark(i))
            iBQ, wBQ, kBQ = wget('BQ')
            b_q = [ps_get('F'), ps_get('F'), ps_get('F')]
            for i in range(3):
                proj(b_q[i], wBQ, kBQ, i * 128, 128, rb_aps(), rb_keys())
            for i in range(3):
                evac_copy(QL32[i], PSB[b_q[i]][:, :], reads=psk(b_q[i]), writes=QLK[i], eng='act')
            rms_rstd(b_q, 384, 1)
            for i in range(3):
                S.op('dve', lambda e, i=i: e.scalar_tensor_tensor(AR[:, 2 + i, :], QL32[i], pvc(PV_QG + i), TMP[:, 1, :], ALU.mult, ALU.mult),
                     reads=tk(1) + QLK[i] + ['pv'], writes=ark(2 + i))
            b_a = ps_get('F')
            proj(b_a, wBQ, kBQ, 384, 16, rb_aps(), rb_keys())
            wrel(iBQ)
            S.op('dve', lambda e: e.tensor_copy(A17[0:16, :], PSB[b_a][0:16, :]), reads=psk(b_a) + ['a17'], writes=['a17'])
            ck_aps = [AR[:, 0, :], AR[:, 1, :]]
            ck_keys = ark(0, 1)
            iW, wW, kW = wget('KVK')
            for h in range(8):
                b = ps_get('F')
                proj(b, wW, kW, h * 128, 128, ck_aps, ck_keys)
                evac_copy(KT[:, h, t0:t0 + TG], PSB[b][:, :], reads=psk(b), writes=[('KT', h, g)], eng='act')
            wrel(iW)
            iW, wW, kW = wget('KVV')
            for tt in range(4):
                for half in range(2):
                    b = ps_get('F')
                    pairs = [(AR[:, kc, tt * P:(tt + 1) * P], wW[:, kc, half * 512:(half + 1) * 512]) for kc in range(2)]
                    mm(PSB[b][:, :], pairs, reads=[kW] + ck_keys, writes=psk(b))
                    evac_copy(V[:, g * 4 + tt, half * 512:(half + 1) * 512], PSB[b][:, :], reads=psk(b), writes=[('V', g * 4 + tt, half)], eng=('act' if half == 0 else None))
            wrel(iW)

            QR = [13, 14, 15, 16, 0, 1, 36, 37]
            qn_aps = [AR[:, 2 + i, :] for i in range(3)]
            qn_keys = ark(2, 3, 4)
            for hb in range(2):
                iW, wW, kW = wget(f'WQB{hb}')
                for hl in range(4):
                    h = hb * 4 + hl
                    b = ps_get('F')
                    proj(b, wW, kW, hl * 128, 128, qn_aps, qn_keys)
                    evac_copy(AR[:, 5 + h, :], PSB[b][:, :], reads=psk(b), writes=ark(5 + h), eng='act')
                for pl in range(2):
                    pr = hb * 2 + pl
                    b_r, b_s = ps_get('F'), ps_get('F')
                    proj(b_r, wW, kW, 512 + pl * 128, 128, qn_aps, qn_keys)
                    proj(b_s, wW, kW, 768 + pl * 128, 128, qn_aps, qn_keys)
                    S.op('dve', lambda e, b_r=b_r: e.tensor_tensor(TMP[:, 5, :], PSB[b_r][:, :], T_c, ALU.mult), reads=psk(b_r) + tk(3), writes=tk(5))
                    S.op('dve', lambda e, b_s=b_s: e.tensor_tensor(TMP[:, 6, :], PSB[b_s][:, :], T_s, ALU.mult), reads=psk(b_s) + tk(4), writes=tk(6))
                    sa, sb_ = QR[2 * pr], QR[2 * pr + 1]
                    S.op('dve', lambda e, sa=sa: e.tensor_tensor(AR[0:64, sa, :], TMP[0:64, 5, :], TMP[0:64, 6, :], ALU.add), reads=tk(5, 6), writes=ark(sa))
                    S.op('act', lambda e, sa=sa: e.memzero(AR[64:128, sa, :]), writes=ark(sa))
                    S.op('dve', lambda e, sb_=sb_: e.tensor_tensor(AR[64:128, sb_, :], TMP[64:128, 5, :], TMP[64:128, 6, :], ALU.add), reads=tk(5, 6), writes=ark(sb_))
                    S.op('act', lambda e, sb_=sb_: e.memzero(AR[0:64, sb_, :]), writes=ark(sb_))
                wrel(iW)
            S.labels.append(('4', S.seq))
            blocks = [(h, kt) for h in range(8) for kt in range(4 * g + 4)]
            nkt = 4 * g + 4

            def score_block(h, kt):
                r = kt - 4 * g
                q0 = P * max(r, 0)
                b = ps_get('S')
                qs = QR[h]

                def fn(e):
                    e.matmul(PSB[b][:, q0:TG], KT[:, h, kt * P:(kt + 1) * P], AR[:, 5 + h, q0:TG], start=True, stop=False)
                    return e.matmul(PSB[b][:, q0:TG], KR[:, kt * P:(kt + 1) * P], AR[:, qs, q0:TG], start=False, stop=True)
                S.op('pe', fn, reads=[('KT', h, kt // 4), ('KR', kt // 4)] + ark(5 + h, qs), writes=psk(b))
                return b, q0, r

            deferred = []
            pendq = [score_block(*blocks[0])]
            for bj in (1, 2):
                if len(blocks) > bj:
                    pendq.append(score_block(*blocks[bj]))
            pt_pos = 0
            for bi, (h, kt) in enumerate(blocks):
                b, q0, r = pendq.pop(0)
                if bi + 3 < len(blocks):
                    pendq.append(score_block(*blocks[bi + 3]))
                pt = 17 + (pt_pos % 3)
                pt_pos += 1
                S.op('act', lambda e, b=b, q0=q0, pt=pt: e.activation(AR[:, pt, q0:TG], PSB[b][:, q0:TG], AF.Exp, scale=ATT_SCALE),
                     reads=psk(b), writes=ark(pt))
                if r >= 0:
                    S.op('dve', lambda e, q0=q0, pt=pt: e.tensor_tensor(AR[:, pt, q0:q0 + P], AR[:, pt, q0:q0 + P], MU[:, :], ALU.mult),
                         reads=ark(pt) + ['mu'], writes=ark(pt))
                b_o = 4 + (h % 2)
                b_s = 6 + (h % 2)
                S.op('pe', lambda e, h=h, kt=kt, q0=q0, pt=pt, b_o=b_o: e.matmul(
                    PSB[b_o][:, q0:TG], V[:, kt, h * P:(h + 1) * P], AR[:, pt, q0:TG], start=(kt == 0), stop=(kt == nkt - 1)),
                    reads=ark(pt) + [('V', kt, h // 4)] + psk(b_o), writes=psk(b_o))
                S.op('pe', lambda e, kt=kt, q0=q0, pt=pt, b_s=b_s: e.matmul(
                    PSB[b_s][:, q0:TG], ONES[:, :], AR[:, pt, q0:TG], start=(kt == 0), stop=(kt == nkt - 1)),
                    reads=ark(pt) + ['ones'] + psk(b_s), writes=psk(b_s))
                for fdef in deferred:
                    fdef()
                deferred.clear()
                if bi == 1:
                    deferred_r32_affine()
                if kt == nkt - 1:
                    def head_end(h=h, b_o=b_o, b_s=b_s):
                        tr = h % 2
                        S.op('act', lambda e: e.activation(TMP[:, tr, :], PSB[b_s][:, :], AF.Ln), reads=psk(b_s), writes=tk(tr))
                        S.op('act', lambda e: e.activation(TMP[:, tr, :], TMP[:, tr, :], AF.Exp, scale=-1.0), reads=tk(tr), writes=tk(tr))
                        S.op('dve', lambda e: e.tensor_tensor(AR[:, 20 + h, :], PSB[b_o][:, :], TMP[:, tr, :], ALU.mult),
                             reads=psk(b_o) + tk(tr), writes=ark(20 + h))
                    deferred.append(head_end)
            for fdef in deferred:
                fdef()
            deferred.clear()

            S.labels.append(('5', S.seq))
            def out_proj_gated(wname, gname, x_slots, gate_col0, add_prev):
                for hb in range(2):
                    iW, wW, kW = wget(f'{wname}{hb}')
                    iG, wG, kG = wget(f'{gname}{hb}')
                    for cl in range(4):
                        c = hb * 4 + cl
                        b_g, b_y = ps_get('G'), ps_get('G')
                        proj(b_g, wG, kG, cl * 128, 128, rb_aps(), rb_keys())
                        proj(b_y, wW, kW, cl * 128, 128, [AR[:, s, :] for s in x_slots], ark(*x_slots))
                        tg_ = 2 + (c % 2)
                        S.op('act', lambda e, b_g=b_g, tg_=tg_, c=c: e.activation(TMP[:, tg_, :], PSB[b_g][:, :], AF.Sigmoid, bias=pvc(PV_BGATE + gate_col0 + c), scale=1.0),
                             reads=psk(b_g) + ['pv'], writes=tk(tg_))
                        if not add_prev:
                            S.op('dve', lambda e, b_y=b_y, tg_=tg_, c=c: e.tensor_tensor(AR[:, 28 + c, :], PSB[b_y][:, :], TMP[:, tg_, :], ALU.mult),
                                 reads=psk(b_y) + tk(tg_), writes=ark(28 + c))
                        else:
                            S.op('dve', lambda e, b_y=b_y, tg_=tg_: e.tensor_tensor(TMP[:, tg_, :], PSB[b_y][:, :], TMP[:, tg_, :], ALU.mult),
                                 reads=psk(b_y) + tk(tg_), writes=tk(tg_))
                            S.op('dve', lambda e, tg_=tg_, c=c: e.tensor_tensor(AR[:, 28 + c, :], TMP[:, tg_, :], AR[:, 28 + c, :], ALU.add),
                                 reads=tk(tg_) + ark(28 + c), writes=ark(28 + c))
                    wrel(iW)
                    wrel(iG)

            out_proj_gated('WOM', 'BGM', list(range(20, 28)), 8, add_prev=False)

            S.labels.append(('6', S.seq))
            iK, wK, kK = wget('BGK')
            EB = lambda h: TMP[:, 4 + h, :]
            ENB = lambda h: XIN[h // 2][:, (h % 2) * 512:(h % 2 + 1) * 512]
            b_zs = [ps_get('G') for _ in range(4)]
            for tt in range(4):
                S.op('pe', lambda e, b_z=b_zs[tt], tt=tt: e.matmul(PSB[b_z][:, :], A17[0:17, tt * P:(tt + 1) * P], WA2[0:17, :], start=True, stop=True),
                     reads=['a17', 'wa2'], writes=psk(b_zs[tt]))

            def gla_chain(tt):
                ta, la = 2 * (tt % 2), 2 * (tt % 2) + 1
                b_z = b_zs[tt]
                S.op('act', lambda e: e.activation(TMP[:, ta, :], PSB[b_z][:, :], AF.Abs), reads=psk(b_z), writes=tk(ta))
                S.op('act', lambda e: e.activation(TMP[:, ta, :], TMP[:, ta, :], AF.Exp, scale=-1.0), reads=tk(ta), writes=tk(ta))
                S.op('act', lambda e: e.activation(TMP[:, ta, :], TMP[:, ta, :], AF.Ln, bias=1.0, scale=1.0), reads=tk(ta), writes=tk(ta))
                S.op('dve', lambda e: e.scalar_tensor_tensor(TMP[:, la, :], PSB[b_z][:, :], 0.0, TMP[:, ta, :], ALU.min, ALU.subtract),
                     reads=psk(b_z) + tk(ta), writes=tk(la))

            def gla_cum(tt):
                tsl = slice(tt * P, (tt + 1) * P)
                la = 2 * (tt % 2) + 1
                LA = TMP[:, la, :]
                b_k = ps_get('G')
                mm(PSB[b_k][:, :], [(Rb[:, kc, tsl], wK[:, kc, :]) for kc in range(8)], reads=[kK] + rb_keys(), writes=psk(b_k))
                b_b = ps_get('G')

                def fnb(e):
                    ins = None
                    for h in range(4):
                        ins = e.matmul(PSB[b_b][:, h * P:(h + 1) * P], LA[:, h * P:(h + 1) * P], UF, start=True, stop=True)
                    return ins
                S.op('pe', fnb, reads=tk(la) + ['cst'], writes=psk(b_b))
                b_r = ps_get('G')
                S.op('pe', lambda e: e.matmul(PSB[b_r][:, :], LF, LA, start=True, stop=True), reads=tk(la) + ['cst'], writes=psk(b_r))
                S.op('act', lambda e: e.activation(TMP[:, 4:8, tsl], PSB[b_b][:, :].rearrange("p (a b) -> p a b", a=4), AF.Exp),
                     reads=psk(b_b), writes=tk(4, 5, 6, 7))
                for jx in range(2):
                    S.op('act', lambda e, jx=jx: e.activation(
                        XIN[jx][:, :].rearrange("p (a b) -> p a b", a=2)[:, :, tsl],
                        PSB[b_b][:, jx * 256:(jx + 1) * 256].rearrange("p (a b) -> p a b", a=2), AF.Exp, scale=-1.0),
                        reads=psk(b_b), writes=[('xin', jx)])
                S.op('act', lambda e: e.activation(LA, PSB[b_r][:, :], AF.Exp), reads=psk(b_r), writes=tk(la))
                S.op('dve', lambda e: e.tensor_tensor(AR[:, 8 + tt, :], PSB[b_k][:, :], LA, ALU.mult),
                     reads=psk(b_k) + tk(la), writes=ark(8 + tt))

            gla_chain(0)
            gla_chain(1)
            gla_cum(0)
            gla_chain(2)
            gla_cum(1)
            gla_chain(3)
            gla_cum(2)
            gla_cum(3)
            for h in range(4):
                b = ps_get('G')
                proj(b, wK, kK, h * 128, 128, rb_aps(), rb_keys())
                S.op('dve', lambda e, b=b, h=h: e.tensor_tensor(AR[:, 4 + h, :], PSB[b][:, :], ENB(h), ALU.mult),
                     reads=psk(b) + [('xin', h // 2)], writes=ark(4 + h))
            wrel(iK)
            iQ, wQ, kQ = wget('BGQ')
            for h in range(4):
                b = ps_get('G')
                proj(b, wQ, kQ, h * 128, 128, rb_aps(), rb_keys())
                S.op('dve', lambda e, b=b, h=h: e.scalar_tensor_tensor(AR[:, h, :], PSB[b][:, :], GLA_QSCALE, EB(h), ALU.mult, ALU.mult),
                     reads=psk(b) + tk(4 + h), writes=ark(h))
            wrel(iQ)
            if g + 1 < ngroups:
                for tt in range(2):
                    S.dma('sp', XIN[tt][:, :], x_d[t0 + TG + tt * P:t0 + TG + (tt + 1) * P, :], writes=[('xin', tt)], sem=f'xin{tt}')
            for blk in range(2):
                iV, wV, kV = wget(f'BGV{blk}')
                for tt in range(4):
                    b = ps_get('G')
                    mm(PSB[b][:, :], [(Rb[:, kc, tt * P:(tt + 1) * P], wV[:, kc, :]) for kc in range(8)], reads=[kV] + rb_keys(), writes=psk(b))
                    evac_copy(AR[:, 12 + 2 * tt + blk, :], PSB[b][:, :], reads=psk(b), writes=ark(12 + 2 * tt + blk))
                wrel(iV)

            sm_pos = 0
            kv_pos = 0
            for pair in range(2):
                steps = [(tt, hl) for tt in range(4) for hl in range(2)]

                def emit_sT(tt, hl, pair=pair):
                    nonlocal sm_pos
                    h = pair * 2 + hl
                    tsl = slice(tt * P, (tt + 1) * P)
                    qd = sm_pos % 4
                    bsT = 4 + 2 * (sm_pos % 2)
                    sm_pos += 1
                    S.op('pe', lambda e: e.matmul(PSB[bsT][:, 0:P], AR[:, 4 + h, tsl], AR[:, h, tsl], start=True, stop=True),
                         reads=ark(4 + h, h), writes=psk(bsT))
                    S.op('dve', lambda e: e.tensor_tensor(SM[:, qd, :], PSB[bsT][:, 0:P], MU[:, :], ALU.mult),
                         reads=psk(bsT) + ['mu'], writes=[('sm', qd)])
                    return qd

                pend_qd = emit_sT(*steps[0])
                for si, (tt, hl) in enumerate(steps):
                    qd = pend_qd
                    if si + 1 < len(steps):
                        pend_qd = emit_sT(*steps[si + 1])
                    tsl = slice(tt * P, (tt + 1) * P)
                    h = pair * 2 + hl
                    vslot = 12 + 2 * tt + (h // 2)
                    vc0 = (h % 2) * 256
                    for dvc in range(2):
                        b_o = hl * 2 + dvc

                        def fno(e, h=h, tsl=tsl, qd=qd, dvc=dvc, b_o=b_o, vslot=vslot, vc0=vc0, tt=tt):
                            e.matmul(PSB[b_o][:, tsl], AR[:, vslot, vc0 + dvc * P:vc0 + (dvc + 1) * P], SM[:, qd, :], start=True, stop=False)
                            return e.matmul(PSB[b_o][:, tsl], Sbf[:, h, dvc * P:(dvc + 1) * P], AR[:, h, tsl], start=False, stop=True)
                        S.op('pe', fno, reads=ark(vslot, h) + [('sm', qd), ('Sbf', h)], writes=[('ps', b_o, tt)])
                    bkv = 5 + 2 * (kv_pos % 2)
                    kv_pos += 1
                    S.op('pe', lambda e, h=h, tt=tt, bkv=bkv, vslot=vslot, vc0=vc0: e.matmul(
                        PSB[bkv][:, 0:256], AR[:, 8 + tt, h * P:(h + 1) * P], AR[:, vslot, vc0:vc0 + 256], start=True, stop=True),
                        reads=ark(8 + tt, vslot), writes=psk(bkv))
                    S.op('dve', lambda e, h=h, tt=tt, bkv=bkv: e.scalar_tensor_tensor(
                        Sst[:, h, :], Sst[:, h, :], TMP[:, 4 + h, tt * P + P - 1:tt * P + P], PSB[bkv][:, 0:256], ALU.mult, ALU.add),
                        reads=[('Sst', h)] + psk(bkv) + tk(4 + h), writes=[('Sst', h)])
                    S.op('act', lambda e, h=h: e.activation(Sbf[:, h, :], Sst[:, h, :], AF.Copy), reads=[('Sst', h)], writes=[('Sbf', h)])
                iR, wR, kR = wget(f'BR{pair}')
                b_stat = 7
                rst = [1, 0]
                chunks = [(hl, dvc) for hl in range(2) for dvc in range(2)]

                def r_proj(ci):
                    hl, dvc = chunks[ci]
                    c = 2 * (pair * 2 + hl) + dvc
                    b_rr = 4 + (ci % 3)
                    proj(b_rr, wR, kR, (c % 4) * 128, 128, rb_aps(), rb_keys())
                    return b_rr
                brs = []
                for hl in range(2):
                    banks = [hl * 2, hl * 2 + 1]
                    sqs = []
                    for i, bq in enumerate(banks):
                        s2 = scr_get()
                        sqs.append(s2)
                        S.op('act', lambda e, bq=bq, s2=s2: e.activation(AR[:, s2, :], PSB[bq][:, :], AF.Square), reads=psk(bq), writes=ark(s2))
                    brs.append(r_proj(hl))
                    for i, s2 in enumerate(sqs):
                        S.op('pe', lambda e, i=i, s2=s2: e.matmul(PSB[b_stat][:, :], ONES[:, :], AR[:, s2, :], start=(i == 0), stop=(i == 1)),
                             reads=ark(s2) + ['ones'] + psk(b_stat), writes=psk(b_stat))
                    tr_ = rst[hl]
                    S.op('act', lambda e, tr_=tr_: e.activation(TMP[:, tr_, :], PSB[b_stat][:, :], AF.Ln, bias=RMS_EPS, scale=1.0 / 256), reads=psk(b_stat), writes=tk(tr_))
                    S.op('act', lambda e, tr_=tr_: e.activation(TMP[:, tr_, :], TMP[:, tr_, :], AF.Exp, scale=-0.5), reads=tk(tr_), writes=tk(tr_))
                brs.append(r_proj(2))
                for ci, (hl, dvc) in enumerate(chunks):
                    h = pair * 2 + hl
                    c = 2 * h + dvc
                    b_o = hl * 2 + dvc
                    tr_ = rst[hl]
                    ts_ = 2 + (c % 2)
                    if ci == 3:
                        brs.append(r_proj(3))
                    b_rr = brs[ci]
                    S.op('act', lambda e, b_rr=b_rr, ts_=ts_: e.activation(TMP[:, ts_, :], PSB[b_rr][:, :], AF.Silu), reads=psk(b_rr), writes=tk(ts_))
                    S.op('dve', lambda e, ts_=ts_, tr_=tr_: e.tensor_tensor(TMP[:, ts_, :], TMP[:, ts_, :], TMP[:, tr_, :], ALU.mult), reads=tk(ts_, tr_), writes=tk(ts_))
                    S.op('dve', lambda e, c=c, b_o=b_o, ts_=ts_: e.scalar_tensor_tensor(AR[:, 20 + c, :], PSB[b_o][:, :], pvc(PV_GLAG + c), TMP[:, ts_, :], ALU.mult, ALU.mult),
                         reads=psk(b_o) + tk(ts_) + ['pv'], writes=ark(20 + c))
                wrel(iR)

            out_proj_gated('WOG', 'BGG', list(range(20, 28)), 0, add_prev=True)

            S.labels.append(('7', S.seq))
            for hb in range(2):
                iW, wW, kW = wget(f'WOUT{hb}')
                for cl in range(4):
                    c = hb * 4 + cl
                    b = ps_get('G')
                    proj(b, wW, kW, cl * 128, 128, [AR[:, 28 + k, :] for k in range(8)], ark(*range(28, 36)))
                    S.op('dve', lambda e, b=b, c=c: e.scalar_tensor_tensor(R32[:, c, :], R32[:, c, :], ALPHA, PSB[b][:, :], ALU.mult, ALU.add),
                         reads=psk(b) + [('R32', c)], writes=[('R32', c)])
                wrel(iW)
            S.labels.append(('8', S.seq))
            layernorm(PV_LN1_G, PV_LN1_B)
            if g + 1 < ngroups:
                x_chain(0)
                x_chain(1)
            S.labels.append(('9', S.seq))
            for hf in range(2):
                for q in range(4):
                    iW, wW, kW = wget(f'FF1_{hf * 4 + q}')
                    fb = [ps_get('F') for _ in range(4)]
                    if hf == 0 and q == 0:
                        proj_kc_major(fb, wW, kW, [0, 128, 256, 384])
                    for jl in range(4):
                        j = q * 4 + jl
                        b = fb[jl]
                        if not (hf == 0 and q == 0):
                            proj(b, wW, kW, jl * 128, 128, rb_aps(), rb_keys())
                        tr = j % 4
                        S.op('act', lambda e, b=b, tr=tr: e.activation(TMP[:, tr, :], PSB[b][:, :], AF.Relu), reads=psk(b), writes=tk(tr))
                        S.op('dve', lambda e, tr=tr, j=j: e.tensor_tensor(AR[:, j, :], TMP[:, tr, :], TMP[:, tr, :], ALU.mult), reads=tk(tr), writes=ark(j))
                    wrel(iW)
                for q in range(4):
                    iW, wW, kW = wget(f'FF2_{hf * 4 + q}')
                    for cl in range(2):
                        c = q * 2 + cl
                        b = ps_get('C')
                        proj(b, wW, kW, cl * 128, 128, [AR[:, j, :] for j in range(16)], ark(*range(16)))
                        if hf == 0:
                            S.op('dve', lambda e, b=b, c=c: e.scalar_tensor_tensor(R32[:, c, :], R32[:, c, :], ALPHA, PSB[b][:, :], ALU.mult, ALU.add),
                                 reads=psk(b) + [('R32', c)], writes=[('R32', c)])
                        else:
                            S.op('dve', lambda e, b=b, c=c: e.tensor_tensor(R32[:, c, :], R32[:, c, :], PSB[b][:, :], ALU.add),
                                 reads=psk(b) + [('R32', c)], writes=[('R32', c)])
                    wrel(iW)
            S.labels.append(('10', S.seq))
            for tt in range(4):
                for half in range(2):
                    bb = 2 * tt + half

                    def fn(e, tt=tt, half=half, bb=bb):
                        ins = None
                        for q in range(4):
                            c = half * 4 + q
                            ins = e.transpose(PSB[bb][:, q * P:(q + 1) * P], R32[:, c, tt * P:(tt + 1) * P], IDENT)
                        return ins
                    S.op('pe', fn, reads=[('R32', half * 4 + q) for q in range(4)] + ['cst'], writes=psk(bb))
            for tt in range(4):
                if tt == 2:
                    yield 'tailA'
                xs = (g * 4 + tt) % 2
                stage = TMP[:, 4 + 2 * xs:6 + 2 * xs, :].rearrange("p a b -> p (a b)")
                skeys = tk(4 + 2 * xs, 5 + 2 * xs)
                lk = [('lst2', xs)]
                bks = [2 * tt, 2 * tt + 1]
                for half in range(2):
                    S.op('dve', lambda e, xs=xs, half=half, bb=bks[half]: e.bn_stats(LST2[:, xs, 6 * half:6 * half + 6], PSB[bb][:, :]), reads=psk(bks[half]) + lk, writes=lk)
                S.op('dve', lambda e, xs=xs: e.bn_aggr(LST2[:, xs, 12:14], LST2[:, xs, 0:12]), reads=lk, writes=lk)
                S.op('act', lambda e, xs=xs: e.activation(LST2[:, xs, 14:15], LST2[:, xs, 13:14], AF.Ln, bias=LN_EPS, scale=1.0), reads=lk, writes=lk)
                S.op('act', lambda e, xs=xs: e.activation(LST2[:, xs, 14:15], LST2[:, xs, 14:15], AF.Exp, scale=-0.5), reads=lk, writes=lk)
                S.op('dve', lambda e, xs=xs: e.scalar_tensor_tensor(LST2[:, xs, 15:16], LST2[:, xs, 12:13], -1.0, LST2[:, xs, 14:15], ALU.mult, ALU.mult), reads=lk, writes=lk)
                for half in range(2):
                    S.op('act', lambda e, xs=xs, half=half, bb=bks[half], stage=stage: e.activation(
                        stage[:, half * 512:(half + 1) * 512], PSB[bb][:, :], AF.Identity, bias=LST2[:, xs, 15:16], scale=LST2[:, xs, 14:15]),
                        reads=psk(bks[half]) + lk, writes=[skeys[half]])
                S.op('dve', lambda e, stage=stage: e.tensor_tensor(stage, stage, G2B2[:, 0, :], ALU.mult), reads=skeys + ['g2'], writes=skeys)
                S.op('dve', lambda e, stage=stage: e.tensor_tensor(stage, stage, G2B2[:, 1, :], ALU.add), reads=skeys + ['b2'], writes=skeys)
                S.dma('pool', out_d[t0 + tt * P:t0 + (tt + 1) * P, :], stage, reads=skeys, sem=f'out{xs}', final=True)

        gens = [do_group(g) for g in range(ngroups)]
        next(gens[0])
        for g in range(ngroups):
            next(gens[g])
            if g + 1 < ngroups:
                next(gens[g + 1])
            for _ in gens[g]:
                pass
        if dbg:
            dbg[0](S, locals(), dbg_d)
        S.emit()
    global LAST_SCHED
    LAST_SCHED = S
    return nc


def _blk(Wm):
    K, n = Wm.shape
    kc = K // P
    return Wm.reshape(kc, P, n).transpose(1, 0, 2).reshape(P, kc * n)


def _col(v):
    v = np.asarray(v, np.float32).reshape(-1)
    return v.reshape(-1, P).T


def prepare_shared(inp):
    w_in = np.asarray(inp['w_in'][0], np.float32)
    w_q_b = np.asarray(inp['w_q_b'][0], np.float32)
    w_kv_b = np.asarray(inp['w_kv_b'][0], np.float32)
    w_o_gla = np.asarray(inp['w_o_gla'][0], np.float32)
    w_o_mla = np.asarray(inp['w_o_mla'][0], np.float32)
    w_out = np.asarray(inp['w_out'][0], np.float32)
    w_ff1 = np.asarray(inp['w_ff1'][0], np.float32)
    w_ff2 = np.asarray(inp['w_ff2'][0], np.float32)
    perm = np.concatenate([np.arange(32, 64), np.arange(0, 32)])
    kr = w_in[:, 3728:3792]
    krs = kr[:, perm]
    GATE0 = 3792
    mats = {}
    mats['BA'] = np.concatenate([kr, kr, krs, krs, w_in[:, 3472:3728]], axis=1)
    mats['BQ'] = np.concatenate([w_in[:, 3088:3472], w_in[:, 3072:3088]], axis=1)
    for hb in range(2):
        hs = range(hb * 4, hb * 4 + 4)
        nope = [w_q_b[:, h * 192:h * 192 + 128] for h in hs]
        rope = [w_q_b[:, h * 192 + 128:(h + 1) * 192] for h in hs]
        sw = [r[:, perm] for r in rope]
        mats[f'WQB{hb}'] = np.concatenate(nope + rope + sw, axis=1)
    mats['KVK'] = np.concatenate([w_kv_b[:, h * 256:h * 256 + 128] for h in range(8)], axis=1)
    mats['KVV'] = np.concatenate([w_kv_b[:, h * 256 + 128:(h + 1) * 256] for h in range(8)], axis=1)
    for hb in range(2):
        cs = slice(hb * 512, (hb + 1) * 512)
        mats[f'WOM{hb}'] = w_o_mla[:, cs]
        mats[f'BGM{hb}'] = w_in[:, GATE0 + 1024 + hb * 512:GATE0 + 1024 + (hb + 1) * 512]
        mats[f'WOG{hb}'] = w_o_gla[:, cs]
        mats[f'BGG{hb}'] = w_in[:, GATE0 + hb * 512:GATE0 + (hb + 1) * 512]
        mats[f'WOUT{hb}'] = w_out[:, cs]
        mats[f'BGV{hb}'] = w_in[:, 1024 + hb * 512:1024 + (hb + 1) * 512]
        mats[f'BR{hb}'] = w_in[:, 2048 + hb * 512:2048 + (hb + 1) * 512]
    mats['BGK'] = w_in[:, 512:1024]
    mats['BGQ'] = w_in[:, 0:512]
    for i in range(8):
        mats[f'FF1_{i}'] = w_ff1[:, i * 512:(i + 1) * 512]
        hf, q = i // 4, i % 4
        mats[f'FF2_{i}'] = w_ff2[hf * 2048:(hf + 1) * 2048, q * 256:(q + 1) * 256]
    ws = np.empty((P, WTOT), np.float32)
    for (name, kc, n), off in zip(BLOCKS, BLK_OFF):
        m = mats[name]
        assert m.shape == (kc * P, n), (name, m.shape)
        ws[:, off:off + kc * n] = _blk(m)
    pv = np.zeros((P, NPV), np.float32)
    pv[:, PV_LNIN_G:PV_LNIN_G + 8] = _col(inp['ln_in_g'])
    pv[:, PV_LNIN_B:PV_LNIN_B + 8] = _col(inp['ln_in_b'])
    pv[:, PV_LN1_G:PV_LN1_G + 8] = _col(inp['ln1_g'])
    pv[:, PV_LN1_B:PV_LN1_B + 8] = _col(inp['ln1_b'])
    pv[:, PV_LN2_G:PV_LN2_G + 8] = _col(inp['ln2_g'])
    pv[:, PV_LN2_B:PV_LN2_B + 8] = _col(inp['ln2_b'])
    pv[:, PV_BGATE:PV_BGATE + 16] = _col(inp['b_gate'])
    pv[:, PV_GLAG:PV_GLAG + 8] = _col(inp['gla_norm_g'])
    pv[:, PV_QG:PV_QG + 3] = _col(inp['q_a_norm_g'])
    pv[:, PV_KVG:PV_KVG + 2] = _col(inp['kv_a_norm_g'])
    inv_freq = (1.0 / (10000.0 ** (np.arange(0, 64, 2, dtype=np.float32) / np.float32(64)))).astype(np.float32)
    pidx = np.arange(P)
    pv[:, PV_INVF] = inv_freq[pidx % 32]
    pv[:, PV_SGN] = np.where((pidx % 64) < 32, -1.0, 1.0)
    wa2 = np.concatenate([np.asarray(inp['w_gla_a2'][0], np.float32), np.asarray(inp['b_gla_a2'], np.float32).reshape(1, 512)], axis=0)
    jj, ii = np.meshgrid(np.arange(P), np.arange(P), indexing='ij')
    cst = np.concatenate([np.eye(P, dtype=np.float32), (jj <= ii).astype(np.float32) / 16.0, (jj > ii).astype(np.float32) / 16.0], axis=1)
    lnout = np.ascontiguousarray(np.stack([np.asarray(inp['ln2_g'], np.float32).reshape(-1), np.asarray(inp['ln2_b'], np.float32).reshape(-1)], axis=0))
    return dict(wstream=ws, pvec=pv, wa2=np.ascontiguousarray(wa2), cst=np.ascontiguousarray(cst), lnout=lnout)


_NC_CACHE = {}
LAST_SCHED = None


def kernel(**inputs):
    x = np.asarray(inputs['x'], np.float32)
    pos = np.asarray(inputs['positions'], np.int32)
    shared = prepare_shared(inputs)
    if 'nc' not in _NC_CACHE:
        _NC_CACHE['nc'] = build_program()
    nc = _NC_CACHE['nc']
    in_maps = []
    for b in range(8):
        m = dict(shared)
        m['x'] = np.ascontiguousarray(x[b])
        m['pos'] = np.ascontiguousarray(pos[b].reshape(1, SEQ))
        in_maps.append(m)
    res = run_bass_kernel_spmd(nc, in_maps, core_ids=list(range(8)))
    return np.stack([np.asarray(r['out'], np.float32) for r in res.results], axis=0)
```
